# Optimizing a Trainium2 kernel written in Bass

```python
import jax, jax.numpy as jnp
from jax import lax
import numpy as np

D_MODEL = 4096
BATCH = 2
SEQ = 8192
DEPTH = 2

HEAD_DIM = 128
N_HEADS = D_MODEL // 256
W_ATTN = N_HEADS * HEAD_DIM
SGU_DIM = 128
SGU_GROUPS = D_MODEL // 256
W_SGU = SGU_GROUPS * SGU_DIM
CHUNK = 128
Q_BLOCK = 128
D_FF = 4 * D_MODEL
N_IN = 3 * W_ATTN + N_HEADS + 2 * W_SGU + 2 * D_MODEL
N_MOD = 6 * D_MODEL
EPS = 1e-6

kernel_name = "fox_sgu_parallel_hybrid_block"


def _rmsnorm(x, g):
    xf = x.astype(jnp.float32)
    y = xf * lax.rsqrt(jnp.mean(xf * xf, axis=-1, keepdims=True) + EPS)
    return (y * g.astype(jnp.float32)).astype(x.dtype)


def _forgetting_attention(q, k, v, log_f):
    B, S, H, Dh = q.shape
    nb = S // Q_BLOCK
    F = jnp.transpose(lax.cumsum(log_f, axis=1), (0, 2, 1))
    qb = q.reshape(B, nb, Q_BLOCK, H, Dh).transpose(1, 0, 2, 3, 4)
    Fq = F.reshape(B, H, nb, Q_BLOCK).transpose(2, 0, 1, 3)
    k_pos = jnp.arange(S)
    scale = HEAD_DIM ** -0.5

    def block(args):
        qi, Fqi, i = args
        s = jnp.einsum('bqhd,bkhd->bhqk', qi, k, preferred_element_type=jnp.float32) * scale
        s = s + Fqi[..., None] - F[:, :, None, :]
        q_pos = i * Q_BLOCK + jnp.arange(Q_BLOCK)
        s = jnp.where(k_pos[None, :] <= q_pos[:, None], s, -jnp.inf)
        p = jax.nn.softmax(s, axis=-1)
        return jnp.einsum('bhqk,bkhd->bqhd', p.astype(v.dtype), v)

    out = lax.map(block, (qb, Fq, jnp.arange(nb)))
    return out.transpose(1, 0, 2, 3, 4).reshape(B, S, H * Dh)


def _chunked_sgu(u, v, g_v, w_s, b_s):
    B, S, W = u.shape
    nc = S // CHUNK
    v = _rmsnorm(v, g_v).reshape(B, nc, CHUNK, SGU_GROUPS, SGU_DIM)
    causal = jnp.tril(jnp.ones((CHUNK, CHUNK), dtype=bool))
    w = jnp.where(causal, w_s, 0).astype(v.dtype)
    mixed = jnp.einsum('gts,bnsgc->bntgc', w, v) + b_s.T[:, :, None].astype(v.dtype)
    return u * mixed.reshape(B, S, W)


def setup_inputs(seed: int = 0) -> dict:
    key = jax.random.key(seed)
    ks = jax.random.split(key, 20)
    L, D = DEPTH, D_MODEL
    nrm = lambda k, shape, s: jax.random.normal(k, shape, jnp.float32) * s
    b_f = (jnp.linspace(1.0, 6.0, N_HEADS, dtype=jnp.float32)[None, :]
           + nrm(ks[6], (L, N_HEADS), 0.1))
    return {
        "x": nrm(ks[0], (BATCH, SEQ, D), 1.0),
        "c": nrm(ks[1], (BATCH, D), 1.0),
        "w_mod": nrm(ks[2], (L, D, N_MOD), 0.5 * D ** -0.5),
        "b_mod": nrm(ks[3], (L, N_MOD), 0.01),
        "g_mix": 1.0 + nrm(ks[4], (L, D), 0.02),
        "w_in": nrm(ks[5], (L, D, N_IN), D ** -0.5),
        "b_f": b_f,
        "g_v": 1.0 + nrm(ks[7], (L, W_SGU), 0.02),
        "w_s": nrm(ks[8], (L, SGU_GROUPS, CHUNK, CHUNK), CHUNK ** -0.5),
        "b_s": 1.0 + nrm(ks[9], (L, SGU_GROUPS, CHUNK), 0.01),
        "w_pa": nrm(ks[10], (L, W_ATTN, D), W_ATTN ** -0.5),
        "w_pm": nrm(ks[11], (L, W_SGU, D), W_SGU ** -0.5),
        "w_o": nrm(ks[12], (L, D, D), D ** -0.5),
        "g_ffn": 1.0 + nrm(ks[13], (L, D), 0.02),
        "w_up": nrm(ks[14], (L, D, D_FF), D ** -0.5),
        "w_down": nrm(ks[15], (L, D_FF, D), D_FF ** -0.5),
        "g_final": 1.0 + nrm(ks[16], (D,), 0.02),
    }


def reference(x, c, w_mod, b_mod, g_mix, w_in, b_f, g_v, w_s, b_s,
              w_pa, w_pm, w_o, g_ffn, w_up, w_down, g_final):
    B, S, D = x.shape
    cond = jax.nn.silu(c)
    splits = np.cumsum([W_ATTN, W_ATTN, W_ATTN, N_HEADS, W_SGU, W_SGU, D_MODEL])[:-0 or None].tolist()
    for l in range(DEPTH):
        mod = cond @ w_mod[l] + b_mod[l]
        sh1, sc1, gt1, sh2, sc2, gt2 = jnp.split(mod[:, None, :], 6, axis=-1)

        h = _rmsnorm(x, g_mix[l]) * (1.0 + sc1) + sh1
        z = h @ w_in[l]
        zq, zk, zv, zf, zu, zg, zga, zgm = jnp.split(z, splits[:7], axis=-1)
        q = zq.reshape(B, S, N_HEADS, HEAD_DIM)
        k = zk.reshape(B, S, N_HEADS, HEAD_DIM)
        v = zv.reshape(B, S, N_HEADS, HEAD_DIM)
        log_f = jax.nn.log_sigmoid(zf.astype(jnp.float32) + b_f[l].astype(jnp.float32))
        a = _forgetting_attention(q, k, v, log_f)
        m = _chunked_sgu(jax.nn.gelu(zu), jax.nn.gelu(zg), g_v[l], w_s[l], b_s[l])
        y = jax.nn.sigmoid(zga) * (a @ w_pa[l]) + jax.nn.sigmoid(zgm) * (m @ w_pm[l])
        x = x + gt1 * (y @ w_o[l])

        h2 = _rmsnorm(x, g_ffn[l]) * (1.0 + sc2) + sh2
        x = x + gt2 * (jnp.square(jax.nn.relu(h2 @ w_up[l])) @ w_down[l])
    return _rmsnorm(x, g_final)
```

```python
import contextlib
import math
import numpy as np
import concourse.bass as bass
import concourse.mybir as mybir
from concourse.bass_utils import run_bass_kernel_spmd

F32 = mybir.dt.float32
BF16 = mybir.dt.bfloat16
AF = mybir.ActivationFunctionType
ALU = mybir.AluOpType

PE, ACT, DVE, POOL, SP = "pe", "act", "dve", "pool", "sp"
ENGS = (PE, ACT, DVE, POOL, SP)
T = 512
NB = 256
EPS = 1e-6
NEG = -30000.0


class Sched:
    def __init__(self, nc, stack):
        self.nc = nc
        self.stack = stack
        self.streams = {e: [] for e in ENGS}
        self.cnt = {}
        self.sems = {}
        self.waited = {e: {} for e in ENGS}
        self.dry = False
        for e in ENGS:
            self.new_sem("E_" + e)

    def new_sem(self, key):
        self.sems[key] = self.stack.enter_context(self.nc.semaphore(key))
        self.cnt[key] = 0
        return key

    def _waits(self, eng, deps):
        out = []
        w = self.waited[eng]
        for d in deps:
            if d is None:
                continue
            k, v = d
            if (eng == PE and k == "E_pe") or v <= 0:
                continue
            if w.get(k, 0) >= v:
                continue
            w[k] = v
            out.append((k, v))
        return out

    def op(self, eng, fn, deps=()):
        if self.dry:
            return None
        ws = self._waits(eng, deps)
        key = "E_" + eng
        self.cnt[key] += 1
        v = self.cnt[key]
        sem = self.sems[key]
        sems = self.sems

        def emit(e, ws=ws, fn=fn, sem=sem):
            for k, val in ws:
                e.wait_ge(sems[k], val)
            fn(e).then_inc(sem, 1)

        self.streams[eng].append(emit)
        return (key, v)

    def dma(self, eng, semkey, out, in_, deps=()):
        if self.dry:
            return None
        ws = self._waits(eng, deps)
        self.cnt[semkey] += 16
        v = self.cnt[semkey]
        sem = self.sems[semkey]
        sems = self.sems

        def emit(e, ws=ws, sem=sem, out=out, in_=in_):
            for k, val in ws:
                e.wait_ge(sems[k], val)
            e.dma_start(out=out, in_=in_).then_inc(sem, 16)

        self.streams[eng].append(emit)
        return (semkey, v)

    def all_done(self, semkey):
        return (semkey, self.cnt[semkey])

    def wait_only(self, eng, deps):
        if self.dry:
            return
        ws = self._waits(eng, deps)
        sems = self.sems

        def emit(e, ws=ws):
            for k, val in ws:
                e.wait_ge(sems[k], val)

        if ws:
            self.streams[eng].append(emit)

    def finish(self):
        nc = self.nc
        streams = self.streams
        with nc.Block() as block:
            @block.tensor
            def _(e):
                for f in streams[PE]:
                    f(e)

            @block.scalar
            def _(e):
                for f in streams[ACT]:
                    f(e)

            @block.vector
            def _(e):
                for f in streams[DVE]:
                    f(e)

            @block.gpsimd
            def _(e):
                for f in streams[POOL]:
                    f(e)

            @block.sync
            def _(e):
                for f in streams[SP]:
                    f(e)


class Buf:
    def __init__(self):
        self.w = {}
        self.r = {}
        self.pend = []

    @staticmethod
    def _m(d, tok):
        if tok is None:
            return
        k, v = tok
        if d.get(k, 0) < v:
            d[k] = v

    def begin(self):
        self.pend = list(self.w.items()) + list(self.r.items())
        self.w = {}
        self.r = {}
        return self.pend

    def wd(self):
        return self.pend

    def wrote(self, tok):
        self._m(self.w, tok)
        return tok

    def rd(self):
        return list(self.w.items())

    def read(self, tok):
        self._m(self.r, tok)
        return tok


class Cfg:
    def __init__(self, D, S, L):
        self.D, self.S, self.L = D, S, L
        self.KC = D // 128
        self.NH = D // 256
        self.WA = self.NH * 128
        self.DFF = 4 * D
        self.FC = min(2048, self.DFF)
        self.NFC = self.DFF // self.FC
        self.KCF = self.FC // 128
        self.NT = S // T
        self.NIN = 3 * self.WA + self.NH + 2 * self.WA + 2 * D
        self.o_q, self.o_k, self.o_v = 0, self.WA, 2 * self.WA
        self.o_f = 3 * self.WA
        self.o_u = self.o_f + self.NH
        self.o_g = self.o_u + self.WA
        self.o_ga = self.o_g + self.WA
        self.o_gm = self.o_ga + D


def build(cfg, dbg=False):
    D, S, L, KC, NH, WA, DFF = cfg.D, cfg.S, cfg.L, cfg.KC, cfg.NH, cfg.WA, cfg.DFF
    FC, NFC, KCF, NT = cfg.FC, cfg.NFC, cfg.KCF, cfg.NT
    KA = WA // 128
    NBLK = S // 128
    SCALE = 128 ** -0.5
    SQD = math.sqrt(D)
    nc = bass.Bass("TRN2", target_bir_lowering=False)

    def din(name, shape, dt=F32):
        return nc.dram_tensor(name, shape, dt, kind="ExternalInput").ap()

    def dscr(name, shape, dt):
        kind = "ExternalOutput" if (dbg and name in ("xT_scr", "Kt_scr", "V_scr", "F_scr")) else "Internal"
        return nc.dram_tensor(name, shape, dt, kind=kind).ap()

    x = din("x", [S, D])
    cT = din("cT", [128, KC])
    w_mod = din("w_mod", [L, D, 6 * D])
    b_modT = din("b_modT", [L, 128, 6 * KC])
    g_mixT = din("g_mixT", [L, 128, KC])
    g_ffnT = din("g_ffnT", [L, 128, KC])
    g_finT = din("g_finT", [128, KC])
    w_in = din("w_in", [L, D, cfg.NIN])
    bf_b = din("bf_b", [L, 128, NH])
    g_vT = din("g_vT", [L, 128, NH])
    w_sT = din("w_sT", [L, 128, NH, 128])
    bs_b = din("bs_b", [L, 128, NH, 128])
    w_pa = din("w_pa", [L, WA, D])
    w_pm = din("w_pm", [L, WA, D])
    w_o = din("w_o", [L, D, D])
    w_up = din("w_up", [L, D, DFF])
    w_down = din("w_down", [L, DFF, D])
    c_ident = din("c_ident", [128, 128])
    c_triadd = din("c_triadd", [128, 128])
    c_trikeep = din("c_trikeep", [128, 128])
    out = nc.dram_tensor("out", [S, D], F32, kind="ExternalOutput").ap()

    xT_scr = dscr("xT_scr", [KC, 128, S], F32)
    Kt_scr = dscr("Kt_scr", [L, NH, 128, S], BF16)
    V_scr = dscr("V_scr", [L, NH, 128, NBLK, 129], BF16)
    F_scr = dscr("F_scr", [L, NH, S], F32)
    dbg_a = nc.dram_tensor("dbg_a", [KA, 128, S], BF16, kind="ExternalOutput").ap() if dbg else None
    dbg_m = nc.dram_tensor("dbg_m", [KA, 128, S], BF16, kind="ExternalOutput").ap() if dbg else None
    Wqk = dscr("Wqk", [L, NH, 128, KC, NB], BF16)
    Wv = dscr("Wv", [L, WA // NB, 128, KC, NB], BF16)
    Wf = dscr("Wf", [L, 128, KC, NH], BF16)
    Wg = dscr("Wg", [L, WA // NB, 128, KC, NB], BF16)
    Wu = dscr("Wu", [L, WA // NB, 128, KC, NB], BF16)
    Wga = dscr("Wga", [L, D // NB, 128, KC, NB], BF16)
    Wgm = dscr("Wgm", [L, D // NB, 128, KC, NB], BF16)
    Wpa = dscr("Wpa", [L, D // NB, 128, KA, NB], BF16)
    Wpm = dscr("Wpm", [L, D // NB, 128, KA, NB], BF16)
    Wo = dscr("Wo", [L, D // NB, 128, KC, NB], BF16)
    Wup = dscr("Wup", [L, DFF // NB, 128, KC, NB], BF16)
    Wdn = dscr("Wdn", [L, NFC, D // NB, 128, KCF, NB], BF16)

    with contextlib.ExitStack() as st:
        S_ = Sched(nc, st)

        def sb(name, shape, dt):
            return st.enter_context(nc.sbuf_tensor(name, shape, dt))

        hT = sb("hT", [128, KC, T], BF16)
        R = sb("R", [128, KC * T], F32)
        xT = R[:, :].rearrange("p (c t) -> p c t", t=T)
        Rb = R[:, :].bitcast(BF16)
        q4 = KC * T // 2
        aT = Rb[:, 0:q4].rearrange("p (c t) -> p c t", t=T)
        mT = Rb[:, q4:2 * q4].rearrange("p (c t) -> p c t", t=T)
        yT = Rb[:, 2 * q4:4 * q4].rearrange("p (c t) -> p c t", t=T)
        gb = Rb[:, 2 * q4:3 * q4].rearrange("p (s n) -> p s n", s=4)
        wsl = [sb(f"wsl{i}", [128, max(KC, KCF), NB], BF16) for i in range(2)]
        M = sb("M", [128, 16384], BF16)
        uff = [M[:, i * 8192:i * 8192 + KCF * T].rearrange("p (c t) -> p c t", t=T) for i in range(2)]
        mo = [0]

        def malloc(n):
            o = mo[0]
            mo[0] = (o + n + 63) // 64 * 64
            assert mo[0] <= 16384
            return M[:, o:o + n]
        Vt = malloc(4 * NH * 129).rearrange("p (s h d) -> p s h d", s=4, h=NH)
        QT = [malloc(T) for i in range(2)]
        KT = [malloc(T) for i in range(2)]
        kbuf = [malloc(T) for i in range(2)]
        vbuf = [malloc(4 * 129).rearrange("p (s d) -> p s d", s=4) for i in range(2)]
        pTb = [malloc(T) for i in range(3)]
        Fq = [malloc(2 * T).bitcast(F32) for i in range(2)]
        negF = sb("negF", [128, NBLK, NH], F32)
        wsTm = sb("wsTm", [128, NH, 128], BF16)
        wsTf = R[:, 0:NH * 128].rearrange("p (g t) -> p g t", t=128)
        bsb = [sb(f"bsb{i}", [128, 128], F32) for i in range(2)]
        identf = sb("identf", [128, 128], F32)
        triadd = sb("triadd", [128, 128], F32)
        trikeep = sb("trikeep", [128, 128], F32)
        onesf = sb("onesf", [128, 128], F32)
        lc = sb("lc", [128, 6, KC], F32)
        modT = sb("modT", [128, L, 6 * KC], F32)
        bmod = sb("bmod", [128, L, 6 * KC], F32)
        gmix = sb("gmix", [128, L, KC], F32)
        gffn = sb("gffn", [128, L, KC], F32)
        gfin = sb("gfin", [128, KC], F32)
        gvT = sb("gvT", [128, L, NH], F32)
        bfb = sb("bfb", [128, L, NH], F32)
        cTs = sb("cTs", [128, KC], F32)
        condT = sb("condT", [128, KC], BF16)
        wfs = sb("wfs", [128, KC, NH], BF16)
        tmpS = [sb(f"tmpS{i}", [128, T], F32) for i in range(2)]
        atok = sb("atok", [128, 4, 128], F32)
        rec = sb("rec", [128, 4], F32)
        stg = [sb(f"stg{i}", [128, T], F32) for i in range(2)]
        ta = [sb(f"ta{i}", [128, T], F32) for i in range(2)]
        tb = [sb(f"tb{i}", [128, T], F32) for i in range(2)]
        uTb = [sb(f"uTb{i}", [128, T], BF16) for i in range(2)]
        rstd = sb("rstd", [128, T], F32)
        sm = sb("sm", [128, 8, NH], F32)
        carry = sb("carry", [128, NH], F32)
        Ftok = sb("Ftok", [128, NH], F32)
        Ftr = sb("Ftr", [NH, 128], F32)
        gss = sb("gss", [128, 8], F32)
        gss2 = sb("gss2", [128, 4, max(WA // NB, 1)], F32)
        epsc = sb("epsc", [128, 2], F32)
        onesb = sb("onesb", [128, 128], BF16)
        trikb = sb("trikb", [128, 128], BF16)
        smb = sb("smb", [128, 2, NH], BF16)

        pg = [st.enter_context(nc.psum_tensor(f"pg{i}", [128, T], F32)) for i in range(4)]
        pst = [st.enter_context(nc.psum_tensor(f"pst{i}", [128, T], F32)) for i in range(2)]
        pO = [st.enter_context(nc.psum_tensor(f"pO{i}", [128, T], F32)) for i in range(2)]
        pgB = [Buf() for _ in range(4)]
        pstB = [Buf() for _ in range(2)]
        pOB = Buf()
        pgi = [0]

        def next_pg():
            i = pgi[0] % 4
            pgi[0] += 1
            return pg[i], pgB[i]

        B = {}

        def bt(name):
            if name not in B:
                B[name] = Buf()
            return B[name]

        for k in ["c0", "wl0", "wl1", "mw0", "mw1", "kst", "vst", "fst", "kl0", "kl1", "vl0", "vl1",
                  "fq0", "fq1", "xl", "xl0", "xl1", "xs", "sl0", "sl1", "ss0", "ss1", "os0", "os1", "cv_xT", "cl0", "cl1", "cl2", "cl3", "cs0", "cs1", "cs2", "cs3"]:
            S_.new_sem(k)
        cvkeys = ["qk", "v", "f", "g", "u", "ga", "gm", "pa", "pm", "o", "up", "dn"]
        for l in range(L):
            for k in cvkeys:
                S_.new_sem(f"cv{l}{k}")

        wseq = []
        CV = {}

        def full_barrier():
            if S_.dry:
                return
            toks = [S_.all_done(k) for k in S_.cnt if not (k.startswith("cv") or k.startswith("mw"))]
            for eng in (PE, ACT, DVE, SP):
                S_.wait_only(eng, toks)
        wstate = {"n": 0, "issued": 0}
        wslB = [Buf(), Buf()]

        def w_issue(n):
            ap_, kc_, cvtok = wseq[n]
            slot = n % 2
            tok = S_.dma(SP, f"wl{slot}", wsl[slot][:, 0:kc_, :], ap_, deps=wslB[slot].begin() + [CV.get(cvtok)])
            wslB[slot].wrote(tok)

        def w_get(ap_, kc_, cvkey):
            if S_.dry:
                wseq.append((ap_, kc_, cvkey))
                return wsl[0], wslB[0]
            n = wstate["n"]
            wstate["n"] += 1
            while wstate["issued"] <= min(n + 1, len(wseq) - 1):
                w_issue(wstate["issued"])
                wstate["issued"] += 1
            return wsl[n % 2], wslB[n % 2]

        def emit_all():
            t_c = []
            for dst, src in [(identf, c_ident), (triadd, c_triadd), (trikeep, c_trikeep), (cTs, cT), (gfin, g_finT)]:
                t_c.append(S_.dma(SP, "c0", dst[:], src))
            for l in range(L):
                t_c.append(S_.dma(SP, "c0", bmod[:, l, :], b_modT[l]))
                t_c.append(S_.dma(SP, "c0", gmix[:, l, :], g_mixT[l]))
                t_c.append(S_.dma(SP, "c0", gffn[:, l, :], g_ffnT[l]))
                t_c.append(S_.dma(SP, "c0", gvT[:, l, :], g_vT[l]))
                t_c.append(S_.dma(SP, "c0", bfb[:, l, :], bf_b[l]))
            c_all = S_.all_done("c0") if not S_.dry else None
            t1 = S_.op(DVE, lambda e: e.memset(onesf[:], 1.0))
            t1 = S_.op(DVE, lambda e: e.memset(onesb[:], 1.0))
            t1 = S_.op(DVE, lambda e: e.memset(epsc[:, 0:1], EPS * D))
            t1 = S_.op(DVE, lambda e: e.memset(epsc[:, 1:2], EPS))
            t_cond = S_.op(ACT, lambda e: e.activation(out=condT[:], in_=cTs[:], func=AF.Silu), deps=[c_all])
            t1 = S_.op(DVE, lambda e: e.tensor_copy(out=trikb[:], in_=trikeep[:]), deps=[c_all])
            bt("consts").wrote(c_all); bt("consts").wrote(t1)
            CD = bt("consts").rd()

            MB = 512
            mslots = [R[:, 0:KC * MB // 2].bitcast(BF16).rearrange("p (c n) -> p c n", n=MB),
                      R[:, KC * MB // 2:KC * MB].bitcast(BF16).rearrange("p (c n) -> p c n", n=MB)]
            mB = [Buf(), Buf()]
            nmb = 6 * D // MB
            pm_, pmB = next_pg()
            for l in range(L):
                wd = pmB.begin()
                for b in range(nmb):
                    s = (l * nmb + b) % 2
                    wdm = mB[s].begin()
                    srcm = w_mod[l][:, b * MB:(b + 1) * MB].rearrange("(c p) n -> p c n", p=128)
                    for k0 in range(0, KC, 8):
                        k1 = min(KC, k0 + 8)
                        S_.dma(POOL, f"mw{s}", mslots[s][:, k0:k1, :], srcm[:, k0:k1, :], deps=wdm)
                    tk = S_.all_done(f"mw{s}") if not S_.dry else None
                    mB[s].wrote(tk)
                    last = None
                    for sub in range(MB // 128):
                        j = b * (MB // 128) + sub
                        for k in range(KC):
                            last = S_.op(PE, lambda e, s=s, sub=sub, k=k, j=j: e.matmul(
                                pm_[:, j:j + 1], lhsT=mslots[s][:, k, sub * 128:(sub + 1) * 128], rhs=condT[:, k:k + 1],
                                start=(k == 0), stop=(k == KC - 1)), deps=[tk, t_cond] + wd)
                    mB[s].read(last)
                    pmB.wrote(last)
                tm = S_.op(DVE, lambda e, l=l: e.tensor_tensor(out=modT[:, l, :], in0=pm_[:, 0:6 * KC], in1=bmod[:, l, :], op=ALU.add),
                           deps=pmB.rd() + CD)
                pmB.read(tm)
                bt("modT").wrote(tm)
            bt("R").read(S_.all_done("E_pe") if not S_.dry else None)

            cvtok = CV
            chunks = []
            for l in range(L):
                def cv(key, dst, src):
                    kc_ = dst.shape[1]
                    for k0 in range(0, kc_, 8):
                        k1 = min(kc_, k0 + 8)
                        chunks.append((dst[:, k0:k1, :], src[:, k0:k1, :]))
                wi = w_in[l]

                def cols(wsrc, c0, n):
                    return wsrc[:, c0:c0 + n].rearrange("(c p) n -> p c n", p=128)
                for h in range(NH):
                    cv("qk", Wqk[l, h][:, :, 0:128], cols(wi, cfg.o_q + h * 128, 128))
                    cv("qk", Wqk[l, h][:, :, 128:256], cols(wi, cfg.o_k + h * 128, 128))
                cv("f", Wf[l], cols(wi, cfg.o_f, NH))
                for b in range(WA // NB):
                    cv("v", Wv[l, b], cols(wi, cfg.o_v + b * NB, NB))
                for b in range(WA // NB):
                    cv("g", Wg[l, b], cols(wi, cfg.o_g + b * NB, NB))
                for b in range(WA // NB):
                    cv("u", Wu[l, b], cols(wi, cfg.o_u + b * NB, NB))
                for b in range(D // NB):
                    cv("ga", Wga[l, b], cols(wi, cfg.o_ga + b * NB, NB))
                    cv("gm", Wgm[l, b], cols(wi, cfg.o_gm + b * NB, NB))
                    cv("pa", Wpa[l, b], cols(w_pa[l], b * NB, NB))
                    cv("pm", Wpm[l, b], cols(w_pm[l], b * NB, NB))
                for b in range(D // NB):
                    cv("o", Wo[l, b], cols(w_o[l], b * NB, NB))
                for b in range(DFF // NB):
                    cv("up", Wup[l, b], cols(w_up[l], b * NB, NB))
                for fcn in range(NFC):
                    for b in range(D // NB):
                        cv("dn", Wdn[l, fcn, b], w_down[l][fcn * FC:(fcn + 1) * FC, b * NB:(b + 1) * NB].rearrange("(c p) n -> p c n", p=128))
            stgs = [wsl[i_ // 2][:, (i_ % 2) * 8:(i_ % 2) * 8 + 8, :] for i_ in range(4)]
            load_tok = {}
            store_tok = [None] * 4
            for n_ in range(len(chunks) + 2):
                if n_ < len(chunks):
                    slot = n_ % 4
                    dst_, src_ = chunks[n_]
                    kn, ncol = dst_.shape[1], dst_.shape[2]
                    load_tok[n_] = S_.dma(POOL, f"cl{slot}", stgs[slot][:, 0:kn, 0:ncol], src_, deps=[store_tok[slot]])
                m_ = n_ - 2
                if 0 <= m_ < len(chunks):
                    slot = m_ % 4
                    dst_, src_ = chunks[m_]
                    kn, ncol = dst_.shape[1], dst_.shape[2]
                    store_tok[slot] = S_.dma(POOL, f"cs{slot}", dst_, stgs[slot][:, 0:kn, 0:ncol], deps=[load_tok.pop(m_)])
            if not S_.dry:
                cdone = [S_.all_done(f"cs{i_}") for i_ in range(4)]
                for eng in (PE, ACT, DVE, SP):
                    S_.wait_only(eng, cdone)

            xs_tok = None
            for blk in range(NBLK):
                half = blk % 2
                xin_ = R[:, half * D:(half + 1) * D] if 2 * D <= KC * T else R[:, 0:D]
                xb = bt(f"xin{half}")
                tl = S_.dma(SP, f"xl{half}", xin_, x[blk * 128:(blk + 1) * 128, :], deps=xb.begin() + bt("R").rd() + list(bt("R").r.items()))
                xb.wrote(tl)
                for c4 in range(0, KC, 4):
                    nn = min(4, KC - c4)
                    p_, pB = next_pg()
                    wd = pB.begin()
                    last = None
                    for c in range(nn):
                        last = S_.op(PE, lambda e, c=c, c4=c4, p_=p_, xin_=xin_: e.transpose(
                            out=p_[:, c * 128:(c + 1) * 128], in_=xin_[:, (c4 + c) * 128:(c4 + c + 1) * 128], identity=identf[:]),
                            deps=[tl] + wd + CD)
                    pB.wrote(last)
                    xb.read(last)
                    s2 = (c4 // 4) % 2
                    sB = bt(f"stg{s2}")
                    te = S_.op(ACT if (c4 // 4) % 2 else DVE,
                               (lambda e, p_=p_, s2=s2, nn=nn: e.copy(out=stg[s2][:, 0:nn * 128], in_=p_[:, 0:nn * 128])) if (c4 // 4) % 2 else
                               (lambda e, p_=p_, s2=s2, nn=nn: e.tensor_copy(out=stg[s2][:, 0:nn * 128], in_=p_[:, 0:nn * 128])),
                               deps=pB.rd() + sB.begin())
                    pB.read(te)
                    sB.wrote(te)
                    td = S_.dma(SP, f"ss{s2}", xT_scr[c4:c4 + nn, :, blk * 128:(blk + 1) * 128].rearrange("c p t -> p c t"),
                                stg[s2][:, 0:nn * 128].rearrange("p (c t) -> p c t", t=128), deps=sB.rd())
                    sB.read(td)
                    xs_tok = td
            if not S_.dry:
                bt("xTscr").wrote(S_.all_done("ss0"))
                bt("xTscr").wrote(S_.all_done("ss1"))
                bt("R").begin()
                bt("R").wrote(S_.all_done("xl0"))
                bt("R").wrote(S_.all_done("xl1"))
                bt("R").read(S_.all_done("E_pe"))

            for l in range(L):
                emit_layer(l, cvtok, CD)

        def rmsnorm_to_hT(gs_idx, sh_idx, lcd, xsrc_deps):
            p_, pB = next_pg()
            wd = pB.begin()
            last = None
            for c in range(KC):
                s2 = c % 2
                tB = bt(f"ta{s2}")
                tsq = S_.op(ACT, lambda e, c=c, s2=s2: e.activation(out=ta[s2][:].bitcast(BF16)[:, 0:T], in_=xT[:, c, :], func=AF.Square),
                            deps=xsrc_deps + tB.begin())
                tB.wrote(tsq)
                last = S_.op(PE, lambda e, c=c, s2=s2, p_=p_: e.matmul(p_[:], lhsT=onesb[:], rhs=ta[s2][:].bitcast(BF16)[:, 0:T], start=(c == 0), stop=(c == KC - 1)),
                             deps=[tsq] + wd)
                tB.read(last)
            pB.wrote(last)
            rB = bt("rstd")
            tr_ = S_.op(ACT, lambda e, p_=p_: e.activation(out=rstd[:], in_=p_[:], func=AF.Sqrt, bias=epsc[:, 0:1], scale=1.0),
                        deps=pB.rd() + rB.begin() + bt("consts").rd())
            pB.read(tr_)
            tr = S_.op(DVE, lambda e: e.reciprocal(out=rstd[:], in_=rstd[:]), deps=[tr_])
            rB.wrote(tr)
            hB = bt("hT")
            hwd = hB.begin()
            for c in range(KC):
                s2 = c % 2
                tB = bt(f"tb{s2}")
                t1 = S_.op(DVE, lambda e, c=c, s2=s2: e.scalar_tensor_tensor(out=tb[s2][:], in0=xT[:, c, :], scalar=lc[:, gs_idx, c:c + 1],
                                                                        in1=rstd[:], op0=ALU.mult, op1=ALU.mult),
                           deps=xsrc_deps + [tr] + lcd + tB.begin())
                tB.wrote(t1)
                t2 = S_.op(ACT, lambda e, c=c, s2=s2: e.activation(out=hT[:, c, :], in_=tb[s2][:], func=AF.Identity, bias=lc[:, sh_idx, c:c + 1], scale=1.0),
                           deps=[t1] + hwd + lcd)
                tB.read(t2)
                hB.wrote(t2)
                bt("R").read(t1)
                rB.read(t1)

        def load_xT_tile(i):
            rB = bt("R")
            wd = rB.begin()
            t0 = i * T
            toks = []
            for c0 in range(0, KC, 8):
                n = min(8, KC - c0)
                toks.append(S_.dma(SP, "xl", xT[:, c0:c0 + n, :], xT_scr[c0:c0 + n, :, t0:t0 + T].rearrange("c p t -> p c t"),
                                   deps=wd + bt("xTscr").rd()))
            tk = S_.all_done("xl") if not S_.dry else None
            rB.wrote(tk)
            return [tk]

        def gemm_fm(wap, kc, cvt, src, srcB, epi, nchunks=2):
            slot, sB = w_get(wap, kc, cvt)
            for cc in range(nchunks):
                p_, pB = next_pg()
                wd = pB.begin()
                last = None
                for k in range(kc):
                    last = S_.op(PE, lambda e, k=k, cc=cc, p_=p_, slot=slot: e.matmul(
                        p_[:], lhsT=slot[:, k, cc * 128:(cc + 1) * 128], rhs=src[:, k, :], start=(k == 0), stop=(k == kc - 1)),
                        deps=sB.rd() + srcB.rd() + wd)
                pB.wrote(last)
                sB.read(last)
                srcB.read(last)
                epi(cc, p_, pB)

        def gemm_tm(wap, kc, cvt, src, srcB, epi, ncols=NB):
            slot, sB = w_get(wap, kc, cvt)
            for sub in range(4):
                p_, pB = next_pg()
                wd = pB.begin()
                last = None
                for k in range(kc):
                    last = S_.op(PE, lambda e, k=k, sub=sub, p_=p_, slot=slot: e.matmul(
                        p_[:, 0:ncols], lhsT=src[:, k, sub * 128:(sub + 1) * 128], rhs=slot[:, k, 0:ncols], start=(k == 0), stop=(k == kc - 1)),
                        deps=sB.rd() + srcB.rd() + wd)
                pB.wrote(last)
                sB.read(last)
                srcB.read(last)
                epi(sub, p_, pB)

        def emit_layer(l, cvtok, CD):
            full_barrier()

            lcB = bt("lc")
            wd = lcB.begin()
            md = bt("modT").rd()
            tks = []
            for (row, sc_i, gsrc) in [(0, 1, gmix), (3, 4, gffn)]:
                t1 = S_.op(DVE, lambda e, row=row, sc_i=sc_i: e.tensor_scalar(out=lc[:, row, :], in0=modT[:, l, sc_i * KC:(sc_i + 1) * KC],
                                                                               scalar1=1.0, scalar2=SQD, op0=ALU.add, op1=ALU.mult), deps=md + wd)
                t2 = S_.op(DVE, lambda e, row=row, gsrc=gsrc: e.tensor_tensor(out=lc[:, row, :], in0=lc[:, row, :], in1=gsrc[:, l, :], op=ALU.mult),
                           deps=[t1] + CD)
                tks.append(t2)
            for (row, mi) in [(1, 0), (2, 2), (4, 3), (5, 5)]:
                tks.append(S_.op(DVE, lambda e, row=row, mi=mi: e.tensor_copy(out=lc[:, row, :], in_=modT[:, l, mi * KC:(mi + 1) * KC]), deps=md + wd))
            for t in tks:
                lcB.wrote(t)
            lcd = lcB.rd()
            sgB = bt("sgw")
            swd = sgB.begin()
            ta_ = S_.dma(SP, "c0", wsTf[:], w_sT[l], deps=swd)
            tf_ = S_.dma(SP, "c0", wfs[:], Wf[l], deps=swd + [CV.get((l, "f"))])
            cd2 = [S_.all_done("c0")] if not S_.dry else []
            for g in range(NH):
                sgB.wrote(S_.op(DVE, lambda e, g=g: e.tensor_tensor(out=wsTm[:, g, :], in0=wsTf[:, g, :], in1=trikeep[:], op=ALU.mult), deps=cd2 + CD + swd))
            sgB.wrote(cd2[0] if cd2 else None)
            t_cz = S_.op(DVE, lambda e: e.memset(carry[:], 0.0), deps=bt("carry").begin())
            bt("carry").wrote(t_cz)
            bt("R").read(S_.all_done("E_dve") if not S_.dry else None)
            for i in range(NT):
                emit_tile(l, i, cvtok, CD, lcd)

        def emit_tile(l, i, cvtok, CD, lcd):
            t0 = i * T
            full_barrier()
            t_one = S_.op(DVE, lambda e: e.memset(Vt[:, :, :, 128:129], 1.0))
            sgd = bt("sgw").rd()
            cv = lambda k: (l, k)
            xd = load_xT_tile(i)
            rmsnorm_to_hT(0, 1, lcd, xd)
            hB = bt("hT")
            rB = bt("R")
            r_free = rB.begin()
            for sub in range(4):
                blk = i * 4 + sub
                p_, pB = next_pg()
                wd = pB.begin()
                last = None
                for k in range(KC):
                    last = S_.op(PE, lambda e, k=k, sub=sub, p_=p_: e.matmul(p_[:, 0:NH], lhsT=hT[:, k, sub * 128:(sub + 1) * 128], rhs=wfs[:, k, :],
                                                                           start=(k == 0), stop=(k == KC - 1)), deps=hB.rd() + sgd + wd)
                pB.wrote(last)
                hB.read(last)
                smB = bt("sm")
                swd = smB.begin()
                a_ = S_.op(DVE, lambda e, p_=p_: e.tensor_tensor(out=sm[:, 0, :], in0=p_[:, 0:NH], in1=bfb[:, l, :], op=ALU.add), deps=pB.rd() + swd + CD)
                pB.read(a_)
                b_ = S_.op(DVE, lambda e: e.tensor_scalar(out=sm[:, 1, :], in0=sm[:, 0, :], scalar1=-60.0, scalar2=None, op0=ALU.max), deps=[a_])
                c_ = S_.op(ACT, lambda e: e.activation(out=sm[:, 2, :], in_=sm[:, 1, :], func=AF.Exp, scale=-1.0), deps=[b_])
                d_ = S_.op(ACT, lambda e: e.activation(out=sm[:, 3, :], in_=sm[:, 2, :], func=AF.Ln, bias=1.0, scale=1.0), deps=[c_])
                f_ = S_.op(DVE, lambda e: e.tensor_scalar(out=sm[:, 5, :], in0=sm[:, 3, :], scalar1=-1.0, scalar2=None, op0=ALU.mult), deps=[d_, b_])
                p2, p2B = next_pg()
                wd2 = p2B.begin()
                h1 = S_.op(DVE, lambda e: e.tensor_copy(out=smb[:, 0, :], in_=sm[:, 5, :]), deps=[f_] + list(bt("smb").r.items()))
                h2 = S_.op(DVE, lambda e: e.tensor_copy(out=sm[:, 6, :], in_=smb[:, 0, :]), deps=[h1])
                h3 = S_.op(DVE, lambda e: e.tensor_tensor(out=sm[:, 7, :], in0=sm[:, 5, :], in1=sm[:, 6, :], op=ALU.subtract), deps=[h2])
                h4 = S_.op(DVE, lambda e: e.tensor_copy(out=smb[:, 1, :], in_=sm[:, 7, :]), deps=[h3])
                m1 = S_.op(PE, lambda e, p2=p2: e.matmul(p2[:, 0:NH], lhsT=trikb[:], rhs=smb[:, 0, :], start=True, stop=False, skip_group_check=True), deps=[h4] + wd2 + CD)
                m1 = S_.op(PE, lambda e, p2=p2: e.matmul(p2[:, 0:NH], lhsT=trikb[:], rhs=smb[:, 1, :], start=False, stop=True, skip_group_check=True), deps=[h4])
                m2 = S_.op(PE, lambda e, p2=p2: e.matmul(p2[:, NH:2 * NH], lhsT=onesb[:], rhs=smb[:, 0, :], start=False, stop=False, skip_group_check=True), deps=[h4])
                m2 = S_.op(PE, lambda e, p2=p2: e.matmul(p2[:, NH:2 * NH], lhsT=onesb[:], rhs=smb[:, 1, :], start=False, stop=True, skip_group_check=True), deps=[h4])
                bt("smb").read(m2)
                p2B.wrote(m2)
                cB = bt("carry")
                fB = bt("Ftok")
                g_ = S_.op(DVE, lambda e, p2=p2: e.tensor_tensor(out=Ftok[:], in0=p2[:, 0:NH], in1=carry[:], op=ALU.add), deps=p2B.rd() + cB.rd() + fB.begin())
                fB.wrote(g_)
                cwd = cB.begin()
                h_ = S_.op(DVE, lambda e, p2=p2: e.tensor_tensor(out=carry[:], in0=p2[:, NH:2 * NH], in1=carry[:], op=ALU.add), deps=[g_] + cwd + p2B.rd())
                cB.wrote(h_)
                p2B.read(h_)
                nB = bt("negF")
                n_ = S_.op(DVE, lambda e, blk=blk: e.tensor_scalar(out=negF[:, blk, :], in0=Ftok[:], scalar1=-1.0, scalar2=None, op0=ALU.mult), deps=[g_] + list(nB.r.items()))
                nB.wrote(n_)
                smB.read(f_); smB.read(m2); smB.wrote(f_)
                p3, p3B = next_pg()
                wd3 = p3B.begin()
                tt = S_.op(PE, lambda e, p3=p3: e.transpose(out=p3[0:NH, 0:128], in_=Ftok[:], identity=identf[:]), deps=[g_] + wd3 + CD)
                p3B.wrote(tt)
                fB.read(tt); fB.read(n_)
                ftB = bt("Ftr")
                tc_ = S_.op(DVE, lambda e, p3=p3: e.tensor_copy(out=Ftr[:], in_=p3[0:NH, 0:128]), deps=[tt] + ftB.begin())
                p3B.read(tc_)
                ftB.wrote(tc_)
                td = S_.dma(SP, "fst", F_scr[l][:, blk * 128:(blk + 1) * 128], Ftr[:], deps=[tc_])
                ftB.read(td)
            f_store = [S_.all_done("fst")] if not S_.dry else []

            vB = bt("Vt")
            vwd = vB.begin() + [t_one]
            vB.wrote(t_one)
            for b in range(WA // NB):
                def epi_v(sub, p_, pB, b=b):
                    eng = ACT if sub % 2 else DVE
                    hh = b * (NB // 128)
                    src_ = p_[:, 0:NB].rearrange("p (h d) -> p h d", d=128)
                    dst_ = Vt[:, sub, hh:hh + NB // 128, 0:128]
                    tk = S_.op(eng, (lambda e: e.copy(out=dst_, in_=src_)) if eng == ACT else (lambda e: e.tensor_copy(out=dst_, in_=src_)), deps=pB.rd() + vwd)
                    pB.read(tk)
                    vB.wrote(tk)
                gemm_tm(Wv[l, b], KC, cv("v"), hT, hB, epi_v)
            for h in range(NH):
                td = S_.dma(SP, "vst", V_scr[l, h][:, i * 4:(i + 1) * 4, :], Vt[:, :, h, :], deps=vB.rd())
                vB.read(S_.all_done("vst") if not S_.dry else None)
            v_store = [S_.all_done("vst")] if not S_.dry else []

            aB = bt("aT")
            awd = r_free
            pending = [None]
            for h in range(NH):
                qs_ = h % 2
                qB, kB = bt(f"QT{qs_}"), bt(f"KT{qs_}")
                qwd, kwd = qB.begin(), kB.begin()

                def epi_qk(cc, p_, pB, qs_=qs_, qB=qB, kB=kB, qwd=qwd, kwd=kwd):
                    if cc == 0:
                        tk = S_.op(ACT, lambda e: e.copy(out=QT[qs_][:], in_=p_[:]), deps=pB.rd() + qwd)
                        qB.wrote(tk)
                    else:
                        tk = S_.op(DVE, lambda e: e.tensor_copy(out=KT[qs_][:], in_=p_[:]), deps=pB.rd() + kwd)
                        kB.wrote(tk)
                    pB.read(tk)
                gemm_fm(Wqk[l, h], KC, cv("qk"), hT, hB, epi_qk)
                td = S_.dma(SP, "kst", Kt_scr[l, h][:, t0:t0 + T], KT[qs_][:], deps=kB.rd())
                kB.read(S_.all_done("kst") if not S_.dry else None)
                if pending[0] is not None:
                    pending[0]()
                    pending[0] = None
                pending[0] = attention(l, i, h, qs_, f_store, v_store, awd, CD)
            pending[0]()
            k_store = None

            gB = bt("gb")
            gwd = r_free
            gsB = bt("gss")
            gswd = gsB.begin()
            tz = S_.op(DVE, lambda e: e.memset(gss2[:], 0.0), deps=gswd)
            nb_g = WA // NB
            for b in range(nb_g):
                def epi_g(sub, p_, pB, b=b):
                    dst_ = gb[:, sub, b * NB:(b + 1) * NB]
                    tk = S_.op(ACT, lambda e: e.activation(out=dst_, in_=p_[:, 0:NB], func=AF.Gelu_apprx_tanh), deps=pB.rd() + gwd)
                    pB.read(tk)
                    gB.wrote(tk)
                    s2 = sub % 2
                    tB = bt(f"ta{s2}")
                    t2 = S_.op(ACT, lambda e: e.activation(out=ta[s2][:, 0:NB], in_=dst_, func=AF.Square, accum_out=gss2[:, sub, b:b + 1]),
                               deps=[tk, tz] + tB.begin())
                    tB.wrote(t2)
                    gsB.wrote(t2)
                gemm_tm(Wg[l, b], KC, cv("g"), hT, hB, epi_g)
            tr0 = S_.op(DVE, lambda e: e.tensor_reduce(out=gss[:, 0:4], in_=gss2[:], axis=mybir.AxisListType.X, op=ALU.add), deps=gsB.rd())
            tr = S_.op(ACT, lambda e: e.activation(out=gss[:, 0:4], in_=gss[:, 0:4], func=AF.Sqrt, bias=epsc[:, 1:2], scale=1.0 / WA), deps=[tr0] + CD)
            tr2 = S_.op(DVE, lambda e: e.reciprocal(out=gss[:, 0:4], in_=gss[:, 0:4]), deps=[tr])
            gsB.wrote(tr2)
            for sub in range(4):
                tk = S_.op(DVE if sub % 2 else ACT,
                           (lambda e, sub=sub: e.tensor_scalar(out=gb[:, sub, :], in0=gb[:, sub, :], scalar1=gss[:, sub:sub + 1], scalar2=None, op0=ALU.mult)) if sub % 2 else
                           (lambda e, sub=sub: e.activation(out=gb[:, sub, :], in_=gb[:, sub, :], func=AF.Copy, scale=gss[:, sub:sub + 1])),
                           deps=[tr2] + gB.rd())
                gB.wrote(tk)
            gsB.read(S_.all_done("E_dve") if not S_.dry else None)
            gsB.read(S_.all_done("E_act") if not S_.dry else None)
            mB = bt("mT")
            mwd = r_free
            for b in range(WA // NB):
                def epi_u(cc, p_, pB, b=b):
                    g = b * 2 + cc
                    s2 = g % 2
                    uB = bt(f"uTb{s2}")
                    tk = S_.op(ACT, lambda e: e.activation(out=uTb[s2][:], in_=p_[:], func=AF.Gelu_apprx_tanh), deps=pB.rd() + uB.begin())
                    pB.read(tk)
                    uB.wrote(tk)
                    p2, p2B = next_pg()
                    wd2 = p2B.begin()
                    last = None
                    for sub in range(4):
                        last = S_.op(PE, lambda e, sub=sub: e.matmul(p2[:, sub * 128:(sub + 1) * 128], lhsT=gb[:, sub, g * 128:(g + 1) * 128], rhs=wsTm[:, g, :],
                                                                     start=True, stop=True), deps=gB.rd() + sgd + wd2)
                    p2B.wrote(last)
                    gB.read(last)
                    tB = bt(f"tb{s2}")
                    bB = bt(f"bsb{s2}")
                    tbs = S_.dma(SP, f"fq{s2}" if False else "c0", bsb[s2][:], bs_b[l][:, g, :], deps=bB.begin())
                    tbs = S_.all_done("c0") if not S_.dry else None
                    bB.wrote(tbs)
                    twd = tB.begin()
                    t1 = None
                    for sub in range(4):
                        t1 = S_.op(DVE, lambda e, sub=sub: e.scalar_tensor_tensor(out=tb[s2][:, sub * 128:(sub + 1) * 128], in0=p2[:, sub * 128:(sub + 1) * 128],
                                                                                   scalar=gvT[:, l, g:g + 1], in1=bsb[s2][:],
                                                                                   op0=ALU.mult, op1=ALU.add), deps=p2B.rd() + twd + [tbs] + CD)
                    bB.read(t1)
                    p2B.read(t1)
                    tB.wrote(t1)
                    t2 = S_.op(DVE, lambda e: e.tensor_tensor(out=mT[:, g, :], in0=tb[s2][:], in1=uTb[s2][:], op=ALU.mult), deps=[t1, tk] + mwd)
                    tB.read(t2)
                    uB.read(t2)
                    mB.wrote(t2)
                gemm_fm(Wu[l, b], KC, cv("u"), hT, hB, epi_u)

            if dbg and l == 0:
                S_.dma(SP, "c0", dbg_a[:, :, t0:t0 + T].rearrange("c p t -> p c t"), aT, deps=bt("aT").rd())
                S_.dma(SP, "c0", dbg_m[:, :, t0:t0 + T].rearrange("c p t -> p c t"), mT, deps=bt("mT").rd())
                full_barrier()
            yB = bt("yT")
            ywd = list(bt("gb").w.items()) + list(bt("gb").r.items()) + r_free
            for b in range(D // NB):
                slots = []
                res = {}
                for (nm, wt, kc_, src, sB_, ck) in [("ga", Wga, KC, hT, hB, "ga"), ("pa", Wpa, KA, aT, aB, "pa"),
                                                    ("gm", Wgm, KC, hT, hB, "gm"), ("pm", Wpm, KA, mT, mB, "pm")]:
                    def epi_p(cc, p_, pB, nm=nm, b=b):
                        c = b * 2 + cc
                        s2 = cc
                        if nm == "ga":
                            tB = bt(f"ta{s2}")
                            tk = S_.op(ACT, lambda e: e.activation(out=ta[s2][:], in_=p_[:], func=AF.Sigmoid), deps=pB.rd() + tB.begin())
                            pB.read(tk); tB.wrote(tk)
                        elif nm == "pa":
                            tB = bt(f"ta{s2}")
                            tk = S_.op(DVE, lambda e: e.tensor_tensor(out=ta[s2][:], in0=p_[:], in1=ta[s2][:], op=ALU.mult), deps=pB.rd() + tB.rd())
                            pB.read(tk); tB.wrote(tk)
                        elif nm == "gm":
                            tB = bt(f"tb{s2}")
                            tk = S_.op(ACT, lambda e: e.activation(out=tb[s2][:], in_=p_[:], func=AF.Sigmoid), deps=pB.rd() + tB.begin())
                            pB.read(tk); tB.wrote(tk)
                        else:
                            tB = bt(f"tb{s2}")
                            tk = S_.op(DVE, lambda e: e.tensor_tensor(out=tb[s2][:], in0=p_[:], in1=tb[s2][:], op=ALU.mult), deps=pB.rd() + tB.rd())
                            pB.read(tk); tB.wrote(tk)
                            tA = bt(f"ta{s2}")
                            t2 = S_.op(DVE, lambda e: e.tensor_tensor(out=yT[:, c, :], in0=ta[s2][:], in1=tb[s2][:], op=ALU.add), deps=[tk] + tA.rd() + ywd)
                            tA.read(t2); tB.read(t2)
                            yB.wrote(t2)
                    gemm_fm(wt[l, b], kc_, cv(ck), src, sB_, epi_p)

            xsB = bt("xTscr")
            for b in range(D // NB):
                def epi_o(cc, p_, pB, b=b):
                    c = b * 2 + cc
                    s2 = c % 2
                    sB = bt(f"stg{s2}")
                    tl = S_.dma(SP, f"sl{s2}", stg[s2][:], xT_scr[c][:, t0:t0 + T], deps=sB.begin() + xsB.rd())
                    sB.wrote(tl)
                    tk = S_.op(DVE, lambda e: e.scalar_tensor_tensor(out=stg[s2][:], in0=p_[:], scalar=lc[:, 2, c:c + 1], in1=stg[s2][:], op0=ALU.mult, op1=ALU.add),
                               deps=pB.rd() + [tl] + lcd)
                    pB.read(tk)
                    sB.wrote(tk)
                    td = S_.dma(SP, f"ss{s2}", xT_scr[c][:, t0:t0 + T], stg[s2][:], deps=[tk])
                    sB.read(td)
                gemm_fm(Wo[l, b], KC, cv("o"), yT, yB, epi_o)
            if not S_.dry:
                xsB.wrote(S_.all_done("ss0")); xsB.wrote(S_.all_done("ss1"))
            rB2 = bt("R")
            full_barrier()
            for nm in ["aT", "mT", "yT", "gb"]:
                for tok in list(bt(nm).r.items()) + list(bt(nm).w.items()):
                    rB2.read(tok)

            xd = load_xT_tile(i)
            rmsnorm_to_hT(3, 4, lcd, xd)
            xB = bt("R")
            for fcn in range(NFC):
                u_ = uff[fcn % 2]
                uB = bt(f"uff{fcn % 2}")
                uwd = uB.begin()
                for b in range(FC // NB):
                    def epi_up(cc, p_, pB, b=b, u_=u_, uB=uB, uwd=uwd):
                        j = b * 2 + cc
                        s2 = j % 2
                        tB = bt(f"ta{s2}")
                        tk = S_.op(ACT, lambda e: e.activation(out=ta[s2][:], in_=p_[:], func=AF.Relu), deps=pB.rd() + tB.begin())
                        pB.read(tk); tB.wrote(tk)
                        t2 = S_.op(DVE, lambda e: e.tensor_tensor(out=u_[:, j, :], in0=ta[s2][:], in1=ta[s2][:], op=ALU.mult), deps=[tk] + uwd)
                        tB.read(t2)
                        uB.wrote(t2)
                    gemm_fm(Wup[l, fcn * (FC // NB) + b], KC, cv("up"), hT, hB, epi_up)
                for b in range(D // NB):
                    def epi_dn(cc, p_, pB, b=b):
                        c = b * 2 + cc
                        tk = S_.op(DVE, lambda e: e.scalar_tensor_tensor(out=xT[:, c, :], in0=p_[:], scalar=lc[:, 5, c:c + 1], in1=xT[:, c, :], op0=ALU.mult, op1=ALU.add),
                                   deps=pB.rd() + xB.rd() + list(xB.r.items()) + lcd)
                        pB.read(tk)
                        xB.wrote(tk)
                    gemm_fm(Wdn[l, fcn, b], KCF, cv("dn"), u_, uB, epi_dn)
            if l < L - 1:
                for c0 in range(0, KC, 8):
                    n = min(8, KC - c0)
                    td = S_.dma(SP, "xs", xT_scr[c0:c0 + n, :, t0:t0 + T].rearrange("c p t -> p c t"), xT[:, c0:c0 + n, :], deps=xB.rd())
                    xB.read(td)
                if not S_.dry:
                    xsB.wrote(S_.all_done("xs"))
            else:
                final_norm_out(i, CD)

        def final_norm_out(i, CD):
            t0 = i * T
            xB = bt("R")
            xdeps = xB.rd()
            p_, pB = next_pg()
            wd = pB.begin()
            last = None
            for c in range(KC):
                s2 = c % 2
                tB = bt(f"ta{s2}")
                tsq = S_.op(ACT, lambda e, c=c, s2=s2: e.activation(out=ta[s2][:].bitcast(BF16)[:, 0:T], in_=xT[:, c, :], func=AF.Square), deps=xdeps + tB.begin())
                tB.wrote(tsq)
                last = S_.op(PE, lambda e, c=c, s2=s2: e.matmul(p_[:], lhsT=onesb[:], rhs=ta[s2][:].bitcast(BF16)[:, 0:T], start=(c == 0), stop=(c == KC - 1)), deps=[tsq] + wd)
                tB.read(last)
            pB.wrote(last)
            rB = bt("rstd")
            tr = S_.op(ACT, lambda e: e.activation(out=rstd[:], in_=p_[:], func=AF.Sqrt, bias=epsc[:, 1:2], scale=1.0 / D), deps=pB.rd() + rB.begin() + CD)
            tr2 = S_.op(DVE, lambda e: e.reciprocal(out=rstd[:], in_=rstd[:]), deps=[tr])
            pB.read(tr)
            rB.wrote(tr2)
            toks = []
            for c in range(KC):
                tk = S_.op(DVE, lambda e, c=c: e.scalar_tensor_tensor(out=xT[:, c, :], in0=xT[:, c, :], scalar=gfin[:, c:c + 1], in1=rstd[:], op0=ALU.mult, op1=ALU.mult),
                           deps=xdeps + [tr2] + CD)
                xB.wrote(tk)
                toks.append(tk)
            rB.read(toks[-1])
            osb = hT[:, :, :].rearrange("p c t -> p (c t)").bitcast(F32)
            hB = bt("hT")
            n_half = (KC * T // 2) // D
            for sub in range(4):
                so = (sub % n_half) * D
                oB = bt(f"osb{sub % n_half}")
                owd = oB.begin() + hB.rd() + list(hB.r.items())
                for c4 in range(0, KC, 4):
                    nn = min(4, KC - c4)
                    p2, p2B = next_pg()
                    wd2 = p2B.begin()
                    last = None
                    for c in range(nn):
                        last = S_.op(PE, lambda e, c=c, c4=c4, p2=p2, sub=sub: e.transpose(out=p2[:, c * 128:(c + 1) * 128], in_=xT[:, c4 + c, sub * 128:(sub + 1) * 128], identity=identf[:]),
                                     deps=xB.rd() + wd2 + CD)
                    p2B.wrote(last)
                    xB.read(last)
                    eng = ACT if (c4 // 4) % 2 else DVE
                    dst_ = osb[:, so + c4 * 128: so + (c4 + nn) * 128]
                    tk = S_.op(eng, (lambda e, p2=p2, dst_=dst_, nn=nn: e.copy(out=dst_, in_=p2[:, 0:nn * 128])) if eng == ACT else
                               (lambda e, p2=p2, dst_=dst_, nn=nn: e.tensor_copy(out=dst_, in_=p2[:, 0:nn * 128])), deps=p2B.rd() + owd)
                    p2B.read(tk)
                    oB.wrote(tk)
                td = S_.dma(SP, f"os{sub % n_half}", out[t0 + sub * 128:t0 + (sub + 1) * 128, :], osb[:, so:so + D], deps=oB.rd())
                oB.read(td)
                hB.read(td)

        def attention(l, i, h, qs_, f_store, v_store, awd, CD):
            t0 = i * T
            qB, kB = bt(f"QT{qs_}"), bt(f"KT{qs_}")
            fs = h % 2
            fqB = bt(f"Fq{fs}")
            tfq = S_.dma(SP, f"fq{fs}", Fq[fs][:], F_scr[l, h, t0:t0 + T].partition_broadcast(128), deps=fqB.begin() + f_store)
            fqB.wrote(tfq)
            items = []
            for j in range(i + 1):
                for kb in range(4):
                    items.append((j, kb))
            nI = len(items)
            loaded = {}

            def load_k(j):
                s = j % 2
                kb_ = bt(f"kbuf{s}")
                tk = S_.dma(SP, f"kl{s}", kbuf[s][:], Kt_scr[l, h][:, j * T:(j + 1) * T], deps=kb_.begin() + ([S_.all_done("kst")] if not S_.dry else []))
                kb_.wrote(tk)

            def load_v(j):
                s = j % 2
                vb_ = bt(f"vbuf{s}")
                tv = S_.dma(SP, f"vl{s}", vbuf[s][:], V_scr[l, h][:, j * 4:(j + 1) * 4, :], deps=vb_.begin() + v_store)
                vb_.wrote(tv)
            if i > 0:
                load_k(0)
                load_v(0)
            pend = []
            oB = pOB
            owd = oB.begin()
            first_pv = [True]
            vB = bt("Vt")
            nB = bt("negF")
            for n in range(nI + 2):
                if n < nI:
                    j, kb = items[n]
                    diag = (j == i)
                    if kb == 0 and j + 1 < i:
                        load_k(j + 1)
                    q0 = kb * 128 if diag else 0
                    s = j % 2
                    if diag:
                        ksrc, ksB = KT[qs_], kB
                    else:
                        ksrc, ksB = kbuf[s], bt(f"kbuf{s}")
                    b2 = n % 2
                    p_, pB = pst[b2], pstB[b2]
                    wd = pB.begin()
                    mm = S_.op(PE, lambda e, p_=p_, ksrc=ksrc, kb=kb, q0=q0: e.matmul(p_[:, q0:T], lhsT=ksrc[:, kb * 128:(kb + 1) * 128], rhs=QT[qs_][:, q0:T], start=True, stop=True),
                               deps=qB.rd() + ksB.rd() + wd)
                    pB.wrote(mm)
                    qB.read(mm); ksB.read(mm)
                    tB = bt(f"tmpS{b2}")
                    t1 = S_.op(DVE, lambda e, p_=p_, b2=b2, q0=q0: e.scalar_tensor_tensor(out=tmpS[b2][:, q0:T], in0=p_[:, q0:T], scalar=SCALE, in1=Fq[fs][:, q0:T],
                                                                                       op0=ALU.mult, op1=ALU.add), deps=[mm, tfq] + tB.begin())
                    pB.read(t1)
                    fqB.read(t1)
                    if diag:
                        t1 = S_.op(DVE, lambda e, b2=b2, q0=q0: e.tensor_tensor(out=tmpS[b2][:, q0:q0 + 128], in0=tmpS[b2][:, q0:q0 + 128], in1=triadd[:], op=ALU.add),
                                   deps=[t1] + CD)
                    tB.wrote(t1)
                    b3 = n % 3
                    ptB = bt(f"pTb{b3}")
                    blk = j * 4 + kb
                    t2 = S_.op(ACT, lambda e, b2=b2, b3=b3, q0=q0, blk=blk: e.activation(out=pTb[b3][:, q0:T], in_=tmpS[b2][:, q0:T], func=AF.Exp,
                                                                                       bias=negF[:, blk, h:h + 1], scale=1.0), deps=[t1] + ptB.begin() + nB.rd())
                    tB.read(t2)
                    nB.read(t2)
                    ptB.wrote(t2)
                    pend.append((j, kb, diag, b3, t2, n))
                if n >= 2 and pend:
                    j, kb, diag, b3, t2, n0 = pend.pop(0)
                    s = j % 2
                    if diag:
                        vsrc = Vt[:, kb, h, 0:128]
                        vsB = vB
                    else:
                        vsrc = vbuf[s][:, kb, 0:128]
                        vsB = bt(f"vbuf{s}")
                    ptB = bt(f"pTb{b3}")
                    q0 = kb * 128 if diag else 0
                    first = (n0 == 0)
                    last_ = (n0 == nI - 1)
                    mm1 = S_.op(PE, lambda e, b3=b3, vsrc=vsrc, q0=q0, first=first, last_=last_: e.matmul(
                        pO[0][:, q0:T], lhsT=vsrc, rhs=pTb[b3][:, q0:T], start=first, stop=last_, skip_group_check=True),
                        deps=[t2] + vsB.rd() + owd)
                    mm2 = S_.op(PE, lambda e, b3=b3, q0=q0, first=first, last_=last_: e.matmul(
                        pO[1][:, q0:T], lhsT=onesb[:], rhs=pTb[b3][:, q0:T], start=first, stop=last_, skip_group_check=True),
                        deps=[t2] + owd + CD)
                    oB.wrote(mm2)
                    ptB.read(mm2)
                    vsB.read(mm1)
                if n < nI and items[n][1] == 1 and items[n][0] + 1 < i:
                    load_v(items[n][0] + 1)

            def finalize():
                aB = bt("aT")
                rcB = bt("tmpS0")
                trec = S_.op(DVE, lambda e: e.reciprocal(out=tmpS[0][:], in_=pO[1][:]), deps=oB.rd() + rcB.begin())
                rcB.wrote(trec)
                tk = S_.op(DVE, lambda e: e.tensor_tensor(out=aT[:, h, :], in0=pO[0][:], in1=tmpS[0][:], op=ALU.mult), deps=[trec] + oB.rd() + awd)
                oB.read(tk)
                rcB.read(tk)
                aB.wrote(tk)
            return finalize

        S_.dry = True
        emit_all()
        S_.dry = False
        B.clear()
        for b_ in pgB + pstB + [pOB] + wslB:
            b_.__init__()
        pgi[0] = 0
        emit_all()
        S_.wait_only(SP, [S_.all_done("os0"), S_.all_done("os1")])
        S_.finish()
    return nc


_CACHE = {}


def _host_inputs(cfg, b, x, c, w_mod, b_mod, g_mix, w_in, b_f, g_v, w_s, b_s, w_pa, w_pm, w_o, g_ffn, w_up, w_down, g_final):
    KC, NH, L = cfg.KC, cfg.NH, cfg.L
    f = lambda a: np.ascontiguousarray(a, dtype=np.float32)
    fm = lambda v, n: f(np.asarray(v).reshape(n, 128).T)
    p = np.arange(128)
    return {
        "x": f(x[b]),
        "cT": fm(c[b], KC),
        "w_mod": f(w_mod),
        "b_modT": f(np.stack([fm(b_mod[l], 6 * KC) for l in range(L)])),
        "g_mixT": f(np.stack([fm(g_mix[l], KC) for l in range(L)])),
        "g_ffnT": f(np.stack([fm(g_ffn[l], KC) for l in range(L)])),
        "g_finT": fm(g_final, KC),
        "w_in": f(w_in),
        "bf_b": f(np.broadcast_to(np.asarray(b_f)[:, None, :], (L, 128, NH))),
        "g_vT": f(np.stack([fm(g_v[l], NH) for l in range(L)])),
        "w_sT": f(np.transpose(np.asarray(w_s), (0, 3, 1, 2))),
        "bs_b": f(np.broadcast_to(np.asarray(b_s)[:, None, :, :], (L, 128, NH, 128))),
        "w_pa": f(w_pa), "w_pm": f(w_pm), "w_o": f(w_o), "w_up": f(w_up), "w_down": f(w_down),
        "c_ident": f(np.eye(128)),
        "c_triadd": f(np.where(p[:, None] > p[None, :], NEG, 0.0)),
        "c_trikeep": f((p[:, None] <= p[None, :]).astype(np.float32)),
    }


def kernel(x, c, w_mod, b_mod, g_mix, w_in, b_f, g_v, w_s, b_s, w_pa, w_pm, w_o, g_ffn, w_up, w_down, g_final):
    x = np.asarray(x)
    Bn, S, D = x.shape
    L = np.asarray(w_mod).shape[0]
    cfg = Cfg(D, S, L)
    key = (D, S, L)
    if key not in _CACHE:
        _CACHE[key] = build(cfg)
    nc = _CACHE[key]
    args = [np.asarray(a) for a in (c, w_mod, b_mod, g_mix, w_in, b_f, g_v, w_s, b_s, w_pa, w_pm, w_o, g_ffn, w_up, w_down, g_final)]
    in_maps = [_host_inputs(cfg, b, x, *args) for b in range(Bn)]
    res = run_bass_kernel_spmd(nc, in_maps, core_ids=list(range(Bn)))
    return np.stack([np.asarray(r["out"]) for r in res.results]).astype(np.float32)
```

```python
import contextlib
import math
import numpy as np
import concourse.bass as bass
import concourse.mybir as mybir
from concourse.bass_utils import run_bass_kernel_spmd

F32 = mybir.dt.float32
BF16 = mybir.dt.bfloat16
AF = mybir.ActivationFunctionType
ALU = mybir.AluOpType

PE, ACT, DVE, POOL, SP = "pe", "act", "dve", "pool", "sp"
ENGS = (PE, ACT, DVE, POOL, SP)
T = 512
NB = 256
EPS = 1e-6
NEG = -30000.0


class Sched:
    def __init__(self, nc, stack):
        self.nc = nc
        self.stack = stack
        self.streams = {e: [] for e in ENGS}
        self.cnt = {}
        self.sems = {}
        self.waited = {e: {} for e in ENGS}
        self.dry = False
        for e in ENGS:
            self.new_sem("E_" + e)

    def new_sem(self, key):
        self.sems[key] = self.stack.enter_context(self.nc.semaphore(key))
        self.cnt[key] = 0
        return key

    def _waits(self, eng, deps):
        out = []
        w = self.waited[eng]
        for d in deps:
            if d is None:
                continue
            k, v = d
            if (eng == PE and k == "E_pe") or v <= 0:
                continue
            if w.get(k, 0) >= v:
                continue
            w[k] = v
            out.append((k, v))
        return out

    def op(self, eng, fn, deps=()):
        if self.dry:
            return None
        ws = self._waits(eng, deps)
        key = "E_" + eng
        self.cnt[key] += 1
        v = self.cnt[key]
        sem = self.sems[key]
        sems = self.sems

        def emit(e, ws=ws, fn=fn, sem=sem):
            for k, val in ws:
                e.wait_ge(sems[k], val)
            fn(e).then_inc(sem, 1)

        self.streams[eng].append(emit)
        return (key, v)

    def dma(self, eng, semkey, out, in_, deps=()):
        if self.dry:
            return None
        ws = self._waits(eng, deps)
        self.cnt[semkey] += 16
        v = self.cnt[semkey]
        sem = self.sems[semkey]
        sems = self.sems

        def emit(e, ws=ws, sem=sem, out=out, in_=in_):
            for k, val in ws:
                e.wait_ge(sems[k], val)
            e.dma_start(out=out, in_=in_).then_inc(sem, 16)

        self.streams[eng].append(emit)
        return (semkey, v)

    def all_done(self, semkey):
        return (semkey, self.cnt[semkey])

    def wait_only(self, eng, deps):
        if self.dry:
            return
        ws = self._waits(eng, deps)
        sems = self.sems

        def emit(e, ws=ws):
            for k, val in ws:
                e.wait_ge(sems[k], val)

        if ws:
            self.streams[eng].append(emit)

    def finish(self):
        nc = self.nc
        streams = self.streams
        with nc.Block() as block:
            @block.tensor
            def _(e):
                for f in streams[PE]:
                    f(e)

            @block.scalar
            def _(e):
                for f in streams[ACT]:
                    f(e)

            @block.vector
            def _(e):
                for f in streams[DVE]:
                    f(e)

            @block.gpsimd
            def _(e):
                for f in streams[POOL]:
                    f(e)

            @block.sync
            def _(e):
                for f in streams[SP]:
                    f(e)


class Buf:
    def __init__(self):
        self.w = {}
        self.r = {}
        self.pend = []

    @staticmethod
    def _m(d, tok):
        if tok is None:
            return
        k, v = tok
        if d.get(k, 0) < v:
            d[k] = v

    def begin(self):
        self.pend = list(self.w.items()) + list(self.r.items())
        self.w = {}
        self.r = {}
        return self.pend

    def wd(self):
        return self.pend

    def wrote(self, tok):
        self._m(self.w, tok)
        return tok

    def rd(self):
        return list(self.w.items())

    def read(self, tok):
        self._m(self.r, tok)
        return tok


class Cfg:
    def __init__(self, D, S, L):
        self.D, self.S, self.L = D, S, L
        self.KC = D // 128
        self.NH = D // 256
        self.WA = self.NH * 128
        self.DFF = 4 * D
        self.FC = min(2048, self.DFF)
        self.NFC = self.DFF // self.FC
        self.KCF = self.FC // 128
        self.NT = S // T
        self.NIN = 3 * self.WA + self.NH + 2 * self.WA + 2 * D
        self.o_q, self.o_k, self.o_v = 0, self.WA, 2 * self.WA
        self.o_f = 3 * self.WA
        self.o_u = self.o_f + self.NH
        self.o_g = self.o_u + self.WA
        self.o_ga = self.o_g + self.WA
        self.o_gm = self.o_ga + D


def build(cfg, dbg=False):
    D, S, L, KC, NH, WA, DFF = cfg.D, cfg.S, cfg.L, cfg.KC, cfg.NH, cfg.WA, cfg.DFF
    FC, NFC, KCF, NT = cfg.FC, cfg.NFC, cfg.KCF, cfg.NT
    KA = WA // 128
    NBLK = S // 128
    SCALE = 128 ** -0.5
    SQD = math.sqrt(D)
    nc = bass.Bass("TRN2", target_bir_lowering=False)

    def din(name, shape, dt=F32):
        return nc.dram_tensor(name, shape, dt, kind="ExternalInput").ap()

    def dscr(name, shape, dt):
        kind = "ExternalOutput" if (dbg and name in ("xT_scr", "Kt_scr", "V_scr", "F_scr")) else "Internal"
        return nc.dram_tensor(name, shape, dt, kind=kind).ap()

    x = din("x", [S, D])
    cT = din("cT", [128, KC])
    w_mod = din("w_mod", [L, D, 6 * D])
    b_modT = din("b_modT", [L, 128, 6 * KC])
    g_mixT = din("g_mixT", [L, 128, KC])
    g_ffnT = din("g_ffnT", [L, 128, KC])
    g_finT = din("g_finT", [128, KC])
    w_in = din("w_in", [L, D, cfg.NIN])
    bf_b = din("bf_b", [L, 128, NH])
    g_vT = din("g_vT", [L, 128, NH])
    w_sT = din("w_sT", [L, 128, NH, 128])
    bs_b = din("bs_b", [L, 128, NH, 128])
    w_pa = din("w_pa", [L, WA, D])
    w_pm = din("w_pm", [L, WA, D])
    w_o = din("w_o", [L, D, D])
    w_up = din("w_up", [L, D, DFF])
    w_down = din("w_down", [L, DFF, D])
    c_ident = din("c_ident", [128, 128])
    c_triadd = din("c_triadd", [128, 128])
    c_trikeep = din("c_trikeep", [128, 128])
    out = nc.dram_tensor("out", [S, D], F32, kind="ExternalOutput").ap()

    xT_scr = dscr("xT_scr", [KC, 128, S], F32)
    Kt_scr = dscr("Kt_scr", [L, NH, 128, S], BF16)
    V_scr = dscr("V_scr", [L, NH, 128, NBLK, 129], BF16)
    F_scr = dscr("F_scr", [L, NH, S], F32)
    dbg_a = nc.dram_tensor("dbg_a", [KA, 128, S], BF16, kind="ExternalOutput").ap() if dbg else None
    dbg_m = nc.dram_tensor("dbg_m", [KA, 128, S], BF16, kind="ExternalOutput").ap() if dbg else None
    Wqk = dscr("Wqk", [L, NH, 128, KC, NB], BF16)
    Wv = dscr("Wv", [L, WA // NB, 128, KC, NB], BF16)
    Wf = dscr("Wf", [L, 128, KC, NH], BF16)
    Wg = dscr("Wg", [L, WA // NB, 128, KC, NB], BF16)
    Wu = dscr("Wu", [L, WA // NB, 128, KC, NB], BF16)
    Wga = dscr("Wga", [L, D // NB, 128, KC, NB], BF16)
    Wgm = dscr("Wgm", [L, D // NB, 128, KC, NB], BF16)
    Wpa = dscr("Wpa", [L, D // NB, 128, KA, NB], BF16)
    Wpm = dscr("Wpm", [L, D // NB, 128, KA, NB], BF16)
    Wo = dscr("Wo", [L, D // NB, 128, KC, NB], BF16)
    Wup = dscr("Wup", [L, DFF // NB, 128, KC, NB], BF16)
    Wdn = dscr("Wdn", [L, NFC, D // NB, 128, KCF, NB], BF16)

    with contextlib.ExitStack() as st:
        S_ = Sched(nc, st)

        def sb(name, shape, dt):
            return st.enter_context(nc.sbuf_tensor(name, shape, dt))

        hT = sb("hT", [128, KC, T], BF16)
        R = sb("R", [128, KC * T], F32)
        xT = R[:, :].rearrange("p (c t) -> p c t", t=T)
        Rb = R[:, :].bitcast(BF16)
        q4 = KC * T // 2
        aT = Rb[:, 0:q4].rearrange("p (c t) -> p c t", t=T)
        mT = Rb[:, q4:2 * q4].rearrange("p (c t) -> p c t", t=T)
        yT = Rb[:, 2 * q4:4 * q4].rearrange("p (c t) -> p c t", t=T)
        gb = Rb[:, 2 * q4:3 * q4].rearrange("p (s n) -> p s n", s=4)
        wsl = [sb(f"wsl{i}", [128, max(KC, KCF), NB], BF16) for i in range(2)]
        M = sb("M", [128, 16384], BF16)
        uff = [M[:, i * 8192:i * 8192 + KCF * T].rearrange("p (c t) -> p c t", t=T) for i in range(2)]
        mo = [0]

        def malloc(n):
            o = mo[0]
            mo[0] = (o + n + 63) // 64 * 64
            assert mo[0] <= 16384
            return M[:, o:o + n]
        Vt = malloc(4 * NH * 129).rearrange("p (s h d) -> p s h d", s=4, h=NH)
        QT = [malloc(T) for i in range(2)]
        KT = [malloc(T) for i in range(2)]
        kbuf = [malloc(T) for i in range(2)]
        vbuf = [malloc(4 * 129).rearrange("p (s d) -> p s d", s=4) for i in range(2)]
        pTb = [malloc(T) for i in range(3)]
        Fq = [malloc(2 * T).bitcast(F32) for i in range(2)]
        negF = sb("negF", [128, NBLK, NH], F32)
        wsTm = sb("wsTm", [128, NH, 128], BF16)
        wsTf = R[:, 0:NH * 128].rearrange("p (g t) -> p g t", t=128)
        bsb = [sb(f"bsb{i}", [128, 128], F32) for i in range(2)]
        identf = sb("identf", [128, 128], F32)
        triadd = sb("triadd", [128, 128], F32)
        trikeep = sb("trikeep", [128, 128], F32)
        lc = sb("lc", [128, 6, KC], F32)
        modT = sb("modT", [128, L, 6 * KC], F32)
        bmod = sb("bmod", [128, L, 6 * KC], F32)
        gmix = sb("gmix", [128, L, KC], F32)
        gffn = sb("gffn", [128, L, KC], F32)
        gfin = sb("gfin", [128, KC], F32)
        gvT = sb("gvT", [128, L, NH], F32)
        bfb = sb("bfb", [128, L, NH], F32)
        cTs = sb("cTs", [128, KC], F32)
        condT = sb("condT", [128, KC], BF16)
        wfs = sb("wfs", [128, KC, NH], BF16)
        tmpS = [sb(f"tmpS{i}", [128, T], F32) for i in range(2)]
        cvs = [sb(f"cvs{i}", [128, 8, NB], BF16) for i in range(2)]
        stg = [sb(f"stg{i}", [128, T], F32) for i in range(2)]
        ta = [sb(f"ta{i}", [128, T], F32) for i in range(2)]
        tb = [sb(f"tb{i}", [128, T], F32) for i in range(2)]
        uTb = [sb(f"uTb{i}", [128, T], BF16) for i in range(2)]
        rstd = sb("rstd", [128, T], F32)
        sm = sb("sm", [128, 8, NH], F32)
        carry = sb("carry", [128, NH], F32)
        Ftok = sb("Ftok", [128, NH], F32)
        Ftr = sb("Ftr", [NH, 128], F32)
        gss = sb("gss", [128, 8], F32)
        gss2 = sb("gss2", [128, 4, max(WA // NB, 1)], F32)
        epsc = sb("epsc", [128, 2], F32)
        onesb = sb("onesb", [128, 128], BF16)
        trikb = sb("trikb", [128, 128], BF16)
        smb = sb("smb", [128, 2, NH], BF16)

        pg = [st.enter_context(nc.psum_tensor(f"pg{i}", [128, T], F32)) for i in range(4)]
        pst = [st.enter_context(nc.psum_tensor(f"pst{i}", [128, T], F32)) for i in range(2)]
        pO = [st.enter_context(nc.psum_tensor(f"pO{i}", [128, T], F32)) for i in range(2)]
        pgB = [Buf() for _ in range(4)]
        pstB = [Buf() for _ in range(2)]
        pOB = Buf()
        pgi = [0]

        def next_pg():
            i = pgi[0] % 4
            pgi[0] += 1
            return pg[i], pgB[i]

        B = {}

        def bt(name):
            if name not in B:
                B[name] = Buf()
            return B[name]

        for k in ["c0", "wl0", "wl1", "mw0", "mw1", "kst", "vst", "fst", "kl0", "kl1", "vl0", "vl1",
                  "fq0", "fq1", "xl", "xl0", "xl1", "xs", "sl0", "sl1", "ss0", "ss1", "os0", "os1", "cv_xT", "cl0", "cl1", "cl2", "cl3", "cs0", "cs1", "cs2", "cs3", "cl4", "cl5", "cs4", "cs5"]:
            S_.new_sem(k)
        cvkeys = ["qk", "v", "f", "g", "u", "ga", "gm", "pa", "pm", "o", "up", "dn"]
        for l in range(L):
            for k in cvkeys:
                S_.new_sem(f"cv{l}{k}")

        wseq = []
        CV = {}

        def full_barrier():
            if S_.dry:
                return
            toks = [S_.all_done(k) for k in S_.cnt if not (k.startswith("cv") or k.startswith("mw") or k.startswith("cl") or k.startswith("cs"))]
            for eng in (PE, ACT, DVE, SP):
                S_.wait_only(eng, toks)
        wstate = {"n": 0, "issued": 0}
        wslB = [Buf(), Buf()]

        def w_issue(n):
            ap_, kc_, cvtok = wseq[n]
            slot = n % 2
            tok = S_.dma(SP, f"wl{slot}", wsl[slot][:, 0:kc_, :], ap_, deps=wslB[slot].begin() + [CV.get(cvtok)])
            wslB[slot].wrote(tok)

        def w_get(ap_, kc_, cvkey):
            if S_.dry:
                wseq.append((ap_, kc_, cvkey))
                return wsl[0], wslB[0]
            n = wstate["n"]
            wstate["n"] += 1
            while wstate["issued"] <= min(n + 1, len(wseq) - 1):
                w_issue(wstate["issued"])
                wstate["issued"] += 1
            return wsl[n % 2], wslB[n % 2]

        def emit_all():
            t_c = []
            for dst, src in [(identf, c_ident), (triadd, c_triadd), (trikeep, c_trikeep), (cTs, cT), (gfin, g_finT)]:
                t_c.append(S_.dma(SP, "c0", dst[:], src))
            for l in range(L):
                t_c.append(S_.dma(SP, "c0", bmod[:, l, :], b_modT[l]))
                t_c.append(S_.dma(SP, "c0", gmix[:, l, :], g_mixT[l]))
                t_c.append(S_.dma(SP, "c0", gffn[:, l, :], g_ffnT[l]))
                t_c.append(S_.dma(SP, "c0", gvT[:, l, :], g_vT[l]))
                t_c.append(S_.dma(SP, "c0", bfb[:, l, :], bf_b[l]))
            c_all = S_.all_done("c0") if not S_.dry else None
            t1 = S_.op(DVE, lambda e: e.memset(onesb[:], 1.0))
            t1 = S_.op(DVE, lambda e: e.memset(epsc[:, 0:1], EPS * D))
            t1 = S_.op(DVE, lambda e: e.memset(epsc[:, 1:2], EPS))
            t_cond = S_.op(ACT, lambda e: e.activation(out=condT[:], in_=cTs[:], func=AF.Silu), deps=[c_all])
            t1 = S_.op(DVE, lambda e: e.tensor_copy(out=trikb[:], in_=trikeep[:]), deps=[c_all])
            bt("consts").wrote(c_all); bt("consts").wrote(t1)
            CD = bt("consts").rd()

            MB = 512
            mslots = [R[:, 0:KC * MB // 2].bitcast(BF16).rearrange("p (c n) -> p c n", n=MB),
                      R[:, KC * MB // 2:KC * MB].bitcast(BF16).rearrange("p (c n) -> p c n", n=MB)]
            mB = [Buf(), Buf()]
            nmb = 6 * D // MB
            pm_, pmB = next_pg()
            for l in range(L):
                wd = pmB.begin()
                for b in range(nmb):
                    s = (l * nmb + b) % 2
                    wdm = mB[s].begin()
                    srcm = w_mod[l][:, b * MB:(b + 1) * MB].rearrange("(c p) n -> p c n", p=128)
                    for k0 in range(0, KC, 8):
                        k1 = min(KC, k0 + 8)
                        S_.dma(POOL, f"mw{s}", mslots[s][:, k0:k1, :], srcm[:, k0:k1, :], deps=wdm)
                    tk = S_.all_done(f"mw{s}") if not S_.dry else None
                    mB[s].wrote(tk)
                    last = None
                    for sub in range(MB // 128):
                        j = b * (MB // 128) + sub
                        for k in range(KC):
                            last = S_.op(PE, lambda e, s=s, sub=sub, k=k, j=j: e.matmul(
                                pm_[:, j:j + 1], lhsT=mslots[s][:, k, sub * 128:(sub + 1) * 128], rhs=condT[:, k:k + 1],
                                start=(k == 0), stop=(k == KC - 1)), deps=[tk, t_cond] + wd)
                    mB[s].read(last)
                    pmB.wrote(last)
                tm = S_.op(DVE, lambda e, l=l: e.tensor_tensor(out=modT[:, l, :], in0=pm_[:, 0:6 * KC], in1=bmod[:, l, :], op=ALU.add),
                           deps=pmB.rd() + CD)
                pmB.read(tm)
                bt("modT").wrote(tm)
            bt("R").read(S_.all_done("E_pe") if not S_.dry else None)

            cvtok = CV
            chunks_l = []
            for l in range(L):
                chunks = []
                chunks_l.append(chunks)
                def cv(key, dst, src):
                    kc_ = dst.shape[1]
                    for k0 in range(0, kc_, 8):
                        k1 = min(kc_, k0 + 8)
                        chunks.append((dst[:, k0:k1, :], src[:, k0:k1, :]))
                wi = w_in[l]

                def cols(wsrc, c0, n):
                    return wsrc[:, c0:c0 + n].rearrange("(c p) n -> p c n", p=128)
                for h in range(NH):
                    cv("qk", Wqk[l, h][:, :, 0:128], cols(wi, cfg.o_q + h * 128, 128))
                    cv("qk", Wqk[l, h][:, :, 128:256], cols(wi, cfg.o_k + h * 128, 128))
                cv("f", Wf[l], cols(wi, cfg.o_f, NH))
                for b in range(WA // NB):
                    cv("v", Wv[l, b], cols(wi, cfg.o_v + b * NB, NB))
                for b in range(WA // NB):
                    cv("g", Wg[l, b], cols(wi, cfg.o_g + b * NB, NB))
                for b in range(WA // NB):
                    cv("u", Wu[l, b], cols(wi, cfg.o_u + b * NB, NB))
                for b in range(D // NB):
                    cv("ga", Wga[l, b], cols(wi, cfg.o_ga + b * NB, NB))
                    cv("gm", Wgm[l, b], cols(wi, cfg.o_gm + b * NB, NB))
                    cv("pa", Wpa[l, b], cols(w_pa[l], b * NB, NB))
                    cv("pm", Wpm[l, b], cols(w_pm[l], b * NB, NB))
                for b in range(D // NB):
                    cv("o", Wo[l, b], cols(w_o[l], b * NB, NB))
                for b in range(DFF // NB):
                    cv("up", Wup[l, b], cols(w_up[l], b * NB, NB))
                for fcn in range(NFC):
                    for b in range(D // NB):
                        cv("dn", Wdn[l, fcn, b], w_down[l][fcn * FC:(fcn + 1) * FC, b * NB:(b + 1) * NB].rearrange("(c p) n -> p c n", p=128))
            stgs = [wsl[i_ // 2][:, (i_ % 2) * 8:(i_ % 2) * 8 + 8, :] for i_ in range(4)]
            load_tok = {}
            store_tok = [None] * 4
            chunks = chunks_l[0]
            for n_ in range(len(chunks) + 2):
                if n_ < len(chunks):
                    slot = n_ % 4
                    dst_, src_ = chunks[n_]
                    kn, ncol = dst_.shape[1], dst_.shape[2]
                    load_tok[n_] = S_.dma(POOL, f"cl{slot}", stgs[slot][:, 0:kn, 0:ncol], src_, deps=[store_tok[slot]])
                m_ = n_ - 2
                if 0 <= m_ < len(chunks):
                    slot = m_ % 4
                    dst_, src_ = chunks[m_]
                    kn, ncol = dst_.shape[1], dst_.shape[2]
                    store_tok[slot] = S_.dma(POOL, f"cs{slot}", dst_, stgs[slot][:, 0:kn, 0:ncol], deps=[load_tok.pop(m_)])
            if not S_.dry:
                cdone = [S_.all_done(f"cs{i_}") for i_ in range(4)]
                for eng in (PE, ACT, DVE, SP):
                    S_.wait_only(eng, cdone)
            load_tok = {}
            store_tok = [None] * 2
            chunks = [c_ for l_ in range(1, L) for c_ in chunks_l[l_]]
            for n_ in range(len(chunks) + 1):
                if n_ < len(chunks):
                    slot = n_ % 2
                    dst_, src_ = chunks[n_]
                    kn, ncol = dst_.shape[1], dst_.shape[2]
                    load_tok[n_] = S_.dma(POOL, f"cl{4 + slot}", cvs[slot][:, 0:kn, 0:ncol], src_, deps=[store_tok[slot]])
                m_ = n_ - 1
                if 0 <= m_ < len(chunks):
                    slot = m_ % 2
                    dst_, src_ = chunks[m_]
                    kn, ncol = dst_.shape[1], dst_.shape[2]
                    store_tok[slot] = S_.dma(POOL, f"cs{4 + slot}", dst_, cvs[slot][:, 0:kn, 0:ncol], deps=[load_tok.pop(m_)])

            xs_tok = None
            for blk in range(NBLK):
                half = blk % 2
                xin_ = R[:, half * D:(half + 1) * D] if 2 * D <= KC * T else R[:, 0:D]
                xb = bt(f"xin{half}")
                tl = S_.dma(SP, f"xl{half}", xin_, x[blk * 128:(blk + 1) * 128, :], deps=xb.begin() + bt("R").rd() + list(bt("R").r.items()))
                xb.wrote(tl)
                for c4 in range(0, KC, 4):
                    nn = min(4, KC - c4)
                    p_, pB = next_pg()
                    wd = pB.begin()
                    last = None
                    for c in range(nn):
                        last = S_.op(PE, lambda e, c=c, c4=c4, p_=p_, xin_=xin_: e.transpose(
                            out=p_[:, c * 128:(c + 1) * 128], in_=xin_[:, (c4 + c) * 128:(c4 + c + 1) * 128], identity=identf[:]),
                            deps=[tl] + wd + CD)
                    pB.wrote(last)
                    xb.read(last)
                    s2 = (c4 // 4) % 2
                    sB = bt(f"stg{s2}")
                    te = S_.op(ACT if (c4 // 4) % 2 else DVE,
                               (lambda e, p_=p_, s2=s2, nn=nn: e.copy(out=stg[s2][:, 0:nn * 128], in_=p_[:, 0:nn * 128])) if (c4 // 4) % 2 else
                               (lambda e, p_=p_, s2=s2, nn=nn: e.tensor_copy(out=stg[s2][:, 0:nn * 128], in_=p_[:, 0:nn * 128])),
                               deps=pB.rd() + sB.begin())
                    pB.read(te)
                    sB.wrote(te)
                    td = S_.dma(SP, f"ss{s2}", xT_scr[c4:c4 + nn, :, blk * 128:(blk + 1) * 128].rearrange("c p t -> p c t"),
                                stg[s2][:, 0:nn * 128].rearrange("p (c t) -> p c t", t=128), deps=sB.rd())
                    sB.read(td)
                    xs_tok = td
            if not S_.dry:
                bt("xTscr").wrote(S_.all_done("ss0"))
                bt("xTscr").wrote(S_.all_done("ss1"))
                bt("R").begin()
                bt("R").wrote(S_.all_done("xl0"))
                bt("R").wrote(S_.all_done("xl1"))
                bt("R").read(S_.all_done("E_pe"))

            for l in range(L):
                emit_layer(l, cvtok, CD)

        def rmsnorm_to_hT(gs_idx, sh_idx, lcd, xsrc_deps):
            p_, pB = next_pg()
            wd = pB.begin()
            last = None
            for c in range(KC):
                s2 = c % 2
                tB = bt(f"ta{s2}")
                tsq = S_.op(ACT, lambda e, c=c, s2=s2: e.activation(out=ta[s2][:].bitcast(BF16)[:, 0:T], in_=xT[:, c, :], func=AF.Square),
                            deps=xsrc_deps + tB.begin())
                tB.wrote(tsq)
                last = S_.op(PE, lambda e, c=c, s2=s2, p_=p_: e.matmul(p_[:], lhsT=onesb[:], rhs=ta[s2][:].bitcast(BF16)[:, 0:T], start=(c == 0), stop=(c == KC - 1)),
                             deps=[tsq] + wd)
                tB.read(last)
            pB.wrote(last)
            rB = bt("rstd")
            tr_ = S_.op(ACT, lambda e, p_=p_: e.activation(out=rstd[:], in_=p_[:], func=AF.Sqrt, bias=epsc[:, 0:1], scale=1.0),
                        deps=pB.rd() + rB.begin() + bt("consts").rd())
            pB.read(tr_)
            tr = S_.op(DVE, lambda e: e.reciprocal(out=rstd[:], in_=rstd[:]), deps=[tr_])
            rB.wrote(tr)
            hB = bt("hT")
            hwd = hB.begin()
            for c in range(KC):
                s2 = c % 2
                tB = bt(f"tb{s2}")
                t1 = S_.op(DVE, lambda e, c=c, s2=s2: e.scalar_tensor_tensor(out=tb[s2][:], in0=xT[:, c, :], scalar=lc[:, gs_idx, c:c + 1],
                                                                        in1=rstd[:], op0=ALU.mult, op1=ALU.mult),
                           deps=xsrc_deps + [tr] + lcd + tB.begin())
                tB.wrote(t1)
                t2 = S_.op(ACT, lambda e, c=c, s2=s2: e.activation(out=hT[:, c, :], in_=tb[s2][:], func=AF.Identity, bias=lc[:, sh_idx, c:c + 1], scale=1.0),
                           deps=[t1] + hwd + lcd)
                tB.read(t2)
                hB.wrote(t2)
                bt("R").read(t1)
                rB.read(t1)

        def load_xT_tile(i):
            rB = bt("R")
            wd = rB.begin()
            t0 = i * T
            toks = []
            for c0 in range(0, KC, 8):
                n = min(8, KC - c0)
                toks.append(S_.dma(SP, "xl", xT[:, c0:c0 + n, :], xT_scr[c0:c0 + n, :, t0:t0 + T].rearrange("c p t -> p c t"),
                                   deps=wd + bt("xTscr").rd()))
            tk = S_.all_done("xl") if not S_.dry else None
            rB.wrote(tk)
            return [tk]

        def gemm_fm(wap, kc, cvt, src, srcB, epi, nchunks=2):
            slot, sB = w_get(wap, kc, cvt)
            for cc in range(nchunks):
                p_, pB = next_pg()
                wd = pB.begin()
                last = None
                for k in range(kc):
                    last = S_.op(PE, lambda e, k=k, cc=cc, p_=p_, slot=slot: e.matmul(
                        p_[:], lhsT=slot[:, k, cc * 128:(cc + 1) * 128], rhs=src[:, k, :], start=(k == 0), stop=(k == kc - 1)),
                        deps=sB.rd() + srcB.rd() + wd)
                pB.wrote(last)
                sB.read(last)
                srcB.read(last)
                epi(cc, p_, pB)

        def gemm_tm(wap, kc, cvt, src, srcB, epi, ncols=NB):
            slot, sB = w_get(wap, kc, cvt)
            for sub in range(4):
                p_, pB = next_pg()
                wd = pB.begin()
                last = None
                for k in range(kc):
                    last = S_.op(PE, lambda e, k=k, sub=sub, p_=p_, slot=slot: e.matmul(
                        p_[:, 0:ncols], lhsT=src[:, k, sub * 128:(sub + 1) * 128], rhs=slot[:, k, 0:ncols], start=(k == 0), stop=(k == kc - 1)),
                        deps=sB.rd() + srcB.rd() + wd)
                pB.wrote(last)
                sB.read(last)
                srcB.read(last)
                epi(sub, p_, pB)

        def emit_layer(l, cvtok, CD):
            full_barrier()
            if l == 1 and not S_.dry:
                cdone = [S_.all_done("cs4"), S_.all_done("cs5")]
                for eng in (PE, ACT, DVE, SP):
                    S_.wait_only(eng, cdone)

            lcB = bt("lc")
            wd = lcB.begin()
            md = bt("modT").rd()
            tks = []
            for (row, sc_i, gsrc) in [(0, 1, gmix), (3, 4, gffn)]:
                t1 = S_.op(DVE, lambda e, row=row, sc_i=sc_i: e.tensor_scalar(out=lc[:, row, :], in0=modT[:, l, sc_i * KC:(sc_i + 1) * KC],
                                                                               scalar1=1.0, scalar2=SQD, op0=ALU.add, op1=ALU.mult), deps=md + wd)
                t2 = S_.op(DVE, lambda e, row=row, gsrc=gsrc: e.tensor_tensor(out=lc[:, row, :], in0=lc[:, row, :], in1=gsrc[:, l, :], op=ALU.mult),
                           deps=[t1] + CD)
                tks.append(t2)
            for (row, mi) in [(1, 0), (2, 2), (4, 3), (5, 5)]:
                tks.append(S_.op(DVE, lambda e, row=row, mi=mi: e.tensor_copy(out=lc[:, row, :], in_=modT[:, l, mi * KC:(mi + 1) * KC]), deps=md + wd))
            for t in tks:
                lcB.wrote(t)
            lcd = lcB.rd()
            sgB = bt("sgw")
            swd = sgB.begin()
            ta_ = S_.dma(SP, "c0", wsTf[:], w_sT[l], deps=swd)
            tf_ = S_.dma(SP, "c0", wfs[:], Wf[l], deps=swd + [CV.get((l, "f"))])
            cd2 = [S_.all_done("c0")] if not S_.dry else []
            for g in range(NH):
                sgB.wrote(S_.op(DVE, lambda e, g=g: e.tensor_tensor(out=wsTm[:, g, :], in0=wsTf[:, g, :], in1=trikeep[:], op=ALU.mult), deps=cd2 + CD + swd))
            sgB.wrote(cd2[0] if cd2 else None)
            t_cz = S_.op(DVE, lambda e: e.memset(carry[:], 0.0), deps=bt("carry").begin())
            bt("carry").wrote(t_cz)
            bt("R").read(S_.all_done("E_dve") if not S_.dry else None)
            for i in range(NT):
                emit_tile(l, i, cvtok, CD, lcd)

        def emit_tile(l, i, cvtok, CD, lcd):
            t0 = i * T
            full_barrier()
            t_one = S_.op(DVE, lambda e: e.memset(Vt[:, :, :, 128:129], 1.0))
            sgd = bt("sgw").rd()
            cv = lambda k: (l, k)
            xd = load_xT_tile(i)
            rmsnorm_to_hT(0, 1, lcd, xd)
            hB = bt("hT")
            rB = bt("R")
            r_free = rB.begin()
            for sub in range(4):
                blk = i * 4 + sub
                p_, pB = next_pg()
                wd = pB.begin()
                last = None
                for k in range(KC):
                    last = S_.op(PE, lambda e, k=k, sub=sub, p_=p_: e.matmul(p_[:, 0:NH], lhsT=hT[:, k, sub * 128:(sub + 1) * 128], rhs=wfs[:, k, :],
                                                                           start=(k == 0), stop=(k == KC - 1)), deps=hB.rd() + sgd + wd)
                pB.wrote(last)
                hB.read(last)
                smB = bt("sm")
                swd = smB.begin()
                a_ = S_.op(DVE, lambda e, p_=p_: e.tensor_tensor(out=sm[:, 0, :], in0=p_[:, 0:NH], in1=bfb[:, l, :], op=ALU.add), deps=pB.rd() + swd + CD)
                pB.read(a_)
                b_ = S_.op(DVE, lambda e: e.tensor_scalar(out=sm[:, 1, :], in0=sm[:, 0, :], scalar1=-60.0, scalar2=None, op0=ALU.max), deps=[a_])
                c_ = S_.op(ACT, lambda e: e.activation(out=sm[:, 2, :], in_=sm[:, 1, :], func=AF.Exp, scale=-1.0), deps=[b_])
                d_ = S_.op(ACT, lambda e: e.activation(out=sm[:, 3, :], in_=sm[:, 2, :], func=AF.Ln, bias=1.0, scale=1.0), deps=[c_])
                f_ = S_.op(DVE, lambda e: e.tensor_scalar(out=sm[:, 5, :], in0=sm[:, 3, :], scalar1=-1.0, scalar2=None, op0=ALU.mult), deps=[d_, b_])
                p2, p2B = next_pg()
                wd2 = p2B.begin()
                h1 = S_.op(DVE, lambda e: e.tensor_copy(out=smb[:, 0, :], in_=sm[:, 5, :]), deps=[f_] + list(bt("smb").r.items()))
                h2 = S_.op(DVE, lambda e: e.tensor_copy(out=sm[:, 6, :], in_=smb[:, 0, :]), deps=[h1])
                h3 = S_.op(DVE, lambda e: e.tensor_tensor(out=sm[:, 7, :], in0=sm[:, 5, :], in1=sm[:, 6, :], op=ALU.subtract), deps=[h2])
                h4 = S_.op(DVE, lambda e: e.tensor_copy(out=smb[:, 1, :], in_=sm[:, 7, :]), deps=[h3])
                m1 = S_.op(PE, lambda e, p2=p2: e.matmul(p2[:, 0:NH], lhsT=trikb[:], rhs=smb[:, 0, :], start=True, stop=False, skip_group_check=True), deps=[h4] + wd2 + CD)
                m1 = S_.op(PE, lambda e, p2=p2: e.matmul(p2[:, 0:NH], lhsT=trikb[:], rhs=smb[:, 1, :], start=False, stop=True, skip_group_check=True), deps=[h4])
                m2 = S_.op(PE, lambda e, p2=p2: e.matmul(p2[:, NH:2 * NH], lhsT=onesb[:], rhs=smb[:, 0, :], start=False, stop=False, skip_group_check=True), deps=[h4])
                m2 = S_.op(PE, lambda e, p2=p2: e.matmul(p2[:, NH:2 * NH], lhsT=onesb[:], rhs=smb[:, 1, :], start=False, stop=True, skip_group_check=True), deps=[h4])
                bt("smb").read(m2)
                p2B.wrote(m2)
                cB = bt("carry")
                fB = bt("Ftok")
                g_ = S_.op(DVE, lambda e, p2=p2: e.tensor_tensor(out=Ftok[:], in0=p2[:, 0:NH], in1=carry[:], op=ALU.add), deps=p2B.rd() + cB.rd() + fB.begin())
                fB.wrote(g_)
                cwd = cB.begin()
                h_ = S_.op(DVE, lambda e, p2=p2: e.tensor_tensor(out=carry[:], in0=p2[:, NH:2 * NH], in1=carry[:], op=ALU.add), deps=[g_] + cwd + p2B.rd())
                cB.wrote(h_)
                p2B.read(h_)
                nB = bt("negF")
                n_ = S_.op(DVE, lambda e, blk=blk: e.tensor_scalar(out=negF[:, blk, :], in0=Ftok[:], scalar1=-1.0, scalar2=None, op0=ALU.mult), deps=[g_] + list(nB.r.items()))
                nB.wrote(n_)
                smB.read(f_); smB.read(m2); smB.wrote(f_)
                p3, p3B = next_pg()
                wd3 = p3B.begin()
                tt = S_.op(PE, lambda e, p3=p3: e.transpose(out=p3[0:NH, 0:128], in_=Ftok[:], identity=identf[:]), deps=[g_] + wd3 + CD)
                p3B.wrote(tt)
                fB.read(tt); fB.read(n_)
                ftB = bt("Ftr")
                tc_ = S_.op(DVE, lambda e, p3=p3: e.tensor_copy(out=Ftr[:], in_=p3[0:NH, 0:128]), deps=[tt] + ftB.begin())
                p3B.read(tc_)
                ftB.wrote(tc_)
                td = S_.dma(SP, "fst", F_scr[l][:, blk * 128:(blk + 1) * 128], Ftr[:], deps=[tc_])
                ftB.read(td)
            f_store = [S_.all_done("fst")] if not S_.dry else []

            vB = bt("Vt")
            vwd = vB.begin() + [t_one]
            vB.wrote(t_one)
            for b in range(WA // NB):
                def epi_v(sub, p_, pB, b=b):
                    eng = ACT if sub % 2 else DVE
                    hh = b * (NB // 128)
                    src_ = p_[:, 0:NB].rearrange("p (h d) -> p h d", d=128)
                    dst_ = Vt[:, sub, hh:hh + NB // 128, 0:128]
                    tk = S_.op(eng, (lambda e: e.copy(out=dst_, in_=src_)) if eng == ACT else (lambda e: e.tensor_copy(out=dst_, in_=src_)), deps=pB.rd() + vwd)
                    pB.read(tk)
                    vB.wrote(tk)
                gemm_tm(Wv[l, b], KC, cv("v"), hT, hB, epi_v)
            for h in range(NH):
                td = S_.dma(SP, "vst", V_scr[l, h][:, i * 4:(i + 1) * 4, :], Vt[:, :, h, :], deps=vB.rd())
                vB.read(S_.all_done("vst") if not S_.dry else None)
            v_store = [S_.all_done("vst")] if not S_.dry else []

            aB = bt("aT")
            awd = r_free
            pending = [None]
            for h in range(NH):
                qs_ = h % 2
                qB, kB = bt(f"QT{qs_}"), bt(f"KT{qs_}")
                qwd, kwd = qB.begin(), kB.begin()

                def epi_qk(cc, p_, pB, qs_=qs_, qB=qB, kB=kB, qwd=qwd, kwd=kwd):
                    if cc == 0:
                        tk = S_.op(ACT, lambda e: e.copy(out=QT[qs_][:], in_=p_[:]), deps=pB.rd() + qwd)
                        qB.wrote(tk)
                    else:
                        tk = S_.op(DVE, lambda e: e.tensor_copy(out=KT[qs_][:], in_=p_[:]), deps=pB.rd() + kwd)
                        kB.wrote(tk)
                    pB.read(tk)
                gemm_fm(Wqk[l, h], KC, cv("qk"), hT, hB, epi_qk)
                td = S_.dma(SP, "kst", Kt_scr[l, h][:, t0:t0 + T], KT[qs_][:], deps=kB.rd())
                kB.read(S_.all_done("kst") if not S_.dry else None)
                if pending[0] is not None:
                    pending[0]()
                    pending[0] = None
                pending[0] = attention(l, i, h, qs_, f_store, v_store, awd, CD)
            pending[0]()
            k_store = None

            gB = bt("gb")
            gwd = r_free
            gsB = bt("gss")
            gswd = gsB.begin()
            tz = S_.op(DVE, lambda e: e.memset(gss2[:], 0.0), deps=gswd)
            nb_g = WA // NB
            for b in range(nb_g):
                def epi_g(sub, p_, pB, b=b):
                    dst_ = gb[:, sub, b * NB:(b + 1) * NB]
                    tk = S_.op(ACT, lambda e: e.activation(out=dst_, in_=p_[:, 0:NB], func=AF.Gelu_apprx_tanh), deps=pB.rd() + gwd)
                    pB.read(tk)
                    gB.wrote(tk)
                    s2 = sub % 2
                    tB = bt(f"ta{s2}")
                    t2 = S_.op(ACT, lambda e: e.activation(out=ta[s2][:, 0:NB], in_=dst_, func=AF.Square, accum_out=gss2[:, sub, b:b + 1]),
                               deps=[tk, tz] + tB.begin())
                    tB.wrote(t2)
                    gsB.wrote(t2)
                gemm_tm(Wg[l, b], KC, cv("g"), hT, hB, epi_g)
            tr0 = S_.op(DVE, lambda e: e.tensor_reduce(out=gss[:, 0:4], in_=gss2[:], axis=mybir.AxisListType.X, op=ALU.add), deps=gsB.rd())
            tr = S_.op(ACT, lambda e: e.activation(out=gss[:, 0:4], in_=gss[:, 0:4], func=AF.Sqrt, bias=epsc[:, 1:2], scale=1.0 / WA), deps=[tr0] + CD)
            tr2 = S_.op(DVE, lambda e: e.reciprocal(out=gss[:, 0:4], in_=gss[:, 0:4]), deps=[tr])
            gsB.wrote(tr2)
            for sub in range(4):
                tk = S_.op(DVE if sub % 2 else ACT,
                           (lambda e, sub=sub: e.tensor_scalar(out=gb[:, sub, :], in0=gb[:, sub, :], scalar1=gss[:, sub:sub + 1], scalar2=None, op0=ALU.mult)) if sub % 2 else
                           (lambda e, sub=sub: e.activation(out=gb[:, sub, :], in_=gb[:, sub, :], func=AF.Copy, scale=gss[:, sub:sub + 1])),
                           deps=[tr2] + gB.rd())
                gB.wrote(tk)
            gsB.read(S_.all_done("E_dve") if not S_.dry else None)
            gsB.read(S_.all_done("E_act") if not S_.dry else None)
            mB = bt("mT")
            mwd = r_free
            for b in range(WA // NB):
                def epi_u(cc, p_, pB, b=b):
                    g = b * 2 + cc
                    s2 = g % 2
                    uB = bt(f"uTb{s2}")
                    tk = S_.op(ACT, lambda e: e.activation(out=uTb[s2][:], in_=p_[:], func=AF.Gelu_apprx_tanh), deps=pB.rd() + uB.begin())
                    pB.read(tk)
                    uB.wrote(tk)
                    p2, p2B = next_pg()
                    wd2 = p2B.begin()
                    last = None
                    for sub in range(4):
                        last = S_.op(PE, lambda e, sub=sub: e.matmul(p2[:, sub * 128:(sub + 1) * 128], lhsT=gb[:, sub, g * 128:(g + 1) * 128], rhs=wsTm[:, g, :],
                                                                     start=True, stop=True), deps=gB.rd() + sgd + wd2)
                    p2B.wrote(last)
                    gB.read(last)
                    tB = bt(f"tb{s2}")
                    bB = bt(f"bsb{s2}")
                    tbs = S_.dma(SP, f"fq{s2}" if False else "c0", bsb[s2][:], bs_b[l][:, g, :], deps=bB.begin())
                    tbs = S_.all_done("c0") if not S_.dry else None
                    bB.wrote(tbs)
                    twd = tB.begin()
                    t1 = None
                    for sub in range(4):
                        t1 = S_.op(DVE, lambda e, sub=sub: e.scalar_tensor_tensor(out=tb[s2][:, sub * 128:(sub + 1) * 128], in0=p2[:, sub * 128:(sub + 1) * 128],
                                                                                   scalar=gvT[:, l, g:g + 1], in1=bsb[s2][:],
                                                                                   op0=ALU.mult, op1=ALU.add), deps=p2B.rd() + twd + [tbs] + CD)
                    bB.read(t1)
                    p2B.read(t1)
                    tB.wrote(t1)
                    t2 = S_.op(DVE, lambda e: e.tensor_tensor(out=mT[:, g, :], in0=tb[s2][:], in1=uTb[s2][:], op=ALU.mult), deps=[t1, tk] + mwd)
                    tB.read(t2)
                    uB.read(t2)
                    mB.wrote(t2)
                gemm_fm(Wu[l, b], KC, cv("u"), hT, hB, epi_u)

            if dbg and l == 0:
                S_.dma(SP, "c0", dbg_a[:, :, t0:t0 + T].rearrange("c p t -> p c t"), aT, deps=bt("aT").rd())
                S_.dma(SP, "c0", dbg_m[:, :, t0:t0 + T].rearrange("c p t -> p c t"), mT, deps=bt("mT").rd())
                full_barrier()
            yB = bt("yT")
            ywd = list(bt("gb").w.items()) + list(bt("gb").r.items()) + r_free
            for b in range(D // NB):
                slots = []
                res = {}
                for (nm, wt, kc_, src, sB_, ck) in [("ga", Wga, KC, hT, hB, "ga"), ("pa", Wpa, KA, aT, aB, "pa"),
                                                    ("gm", Wgm, KC, hT, hB, "gm"), ("pm", Wpm, KA, mT, mB, "pm")]:
                    def epi_p(cc, p_, pB, nm=nm, b=b):
                        c = b * 2 + cc
                        s2 = cc
                        if nm == "ga":
                            tB = bt(f"ta{s2}")
                            tk = S_.op(ACT, lambda e: e.activation(out=ta[s2][:], in_=p_[:], func=AF.Sigmoid), deps=pB.rd() + tB.begin())
                            pB.read(tk); tB.wrote(tk)
                        elif nm == "pa":
                            tB = bt(f"ta{s2}")
                            tk = S_.op(DVE, lambda e: e.tensor_tensor(out=ta[s2][:], in0=p_[:], in1=ta[s2][:], op=ALU.mult), deps=pB.rd() + tB.rd())
                            pB.read(tk); tB.wrote(tk)
                        elif nm == "gm":
                            tB = bt(f"tb{s2}")
                            tk = S_.op(ACT, lambda e: e.activation(out=tb[s2][:], in_=p_[:], func=AF.Sigmoid), deps=pB.rd() + tB.begin())
                            pB.read(tk); tB.wrote(tk)
                        else:
                            tB = bt(f"tb{s2}")
                            tk = S_.op(DVE, lambda e: e.tensor_tensor(out=tb[s2][:], in0=p_[:], in1=tb[s2][:], op=ALU.mult), deps=pB.rd() + tB.rd())
                            pB.read(tk); tB.wrote(tk)
                            tA = bt(f"ta{s2}")
                            t2 = S_.op(DVE, lambda e: e.tensor_tensor(out=yT[:, c, :], in0=ta[s2][:], in1=tb[s2][:], op=ALU.add), deps=[tk] + tA.rd() + ywd)
                            tA.read(t2); tB.read(t2)
                            yB.wrote(t2)
                    gemm_fm(wt[l, b], kc_, cv(ck), src, sB_, epi_p)

            xsB = bt("xTscr")
            for b in range(D // NB):
                def epi_o(cc, p_, pB, b=b):
                    c = b * 2 + cc
                    s2 = c % 2
                    sB = bt(f"stg{s2}")
                    tl = S_.dma(SP, f"sl{s2}", stg[s2][:], xT_scr[c][:, t0:t0 + T], deps=sB.begin() + xsB.rd())
                    sB.wrote(tl)
                    tk = S_.op(DVE, lambda e: e.scalar_tensor_tensor(out=stg[s2][:], in0=p_[:], scalar=lc[:, 2, c:c + 1], in1=stg[s2][:], op0=ALU.mult, op1=ALU.add),
                               deps=pB.rd() + [tl] + lcd)
                    pB.read(tk)
                    sB.wrote(tk)
                    td = S_.dma(SP, f"ss{s2}", xT_scr[c][:, t0:t0 + T], stg[s2][:], deps=[tk])
                    sB.read(td)
                gemm_fm(Wo[l, b], KC, cv("o"), yT, yB, epi_o)
            if not S_.dry:
                xsB.wrote(S_.all_done("ss0")); xsB.wrote(S_.all_done("ss1"))
            rB2 = bt("R")
            full_barrier()
            for nm in ["aT", "mT", "yT", "gb"]:
                for tok in list(bt(nm).r.items()) + list(bt(nm).w.items()):
                    rB2.read(tok)

            xd = load_xT_tile(i)
            rmsnorm_to_hT(3, 4, lcd, xd)
            xB = bt("R")
            for fcn in range(NFC):
                u_ = uff[fcn % 2]
                uB = bt(f"uff{fcn % 2}")
                uwd = uB.begin()
                for b in range(FC // NB):
                    def epi_up(cc, p_, pB, b=b, u_=u_, uB=uB, uwd=uwd):
                        j = b * 2 + cc
                        s2 = j % 2
                        tB = bt(f"ta{s2}")
                        tk = S_.op(ACT, lambda e: e.activation(out=ta[s2][:], in_=p_[:], func=AF.Relu), deps=pB.rd() + tB.begin())
                        pB.read(tk); tB.wrote(tk)
                        t2 = S_.op(DVE, lambda e: e.tensor_tensor(out=u_[:, j, :], in0=ta[s2][:], in1=ta[s2][:], op=ALU.mult), deps=[tk] + uwd)
                        tB.read(t2)
                        uB.wrote(t2)
                    gemm_fm(Wup[l, fcn * (FC // NB) + b], KC, cv("up"), hT, hB, epi_up)
                for b in range(D // NB):
                    def epi_dn(cc, p_, pB, b=b):
                        c = b * 2 + cc
                        tk = S_.op(DVE, lambda e: e.scalar_tensor_tensor(out=xT[:, c, :], in0=p_[:], scalar=lc[:, 5, c:c + 1], in1=xT[:, c, :], op0=ALU.mult, op1=ALU.add),
                                   deps=pB.rd() + xB.rd() + list(xB.r.items()) + lcd)
                        pB.read(tk)
                        xB.wrote(tk)
                    gemm_fm(Wdn[l, fcn, b], KCF, cv("dn"), u_, uB, epi_dn)
            if l < L - 1:
                for c0 in range(0, KC, 8):
                    n = min(8, KC - c0)
                    td = S_.dma(SP, "xs", xT_scr[c0:c0 + n, :, t0:t0 + T].rearrange("c p t -> p c t"), xT[:, c0:c0 + n, :], deps=xB.rd())
                    xB.read(td)
                if not S_.dry:
                    xsB.wrote(S_.all_done("xs"))
            else:
                final_norm_out(i, CD)

        def final_norm_out(i, CD):
            t0 = i * T
            xB = bt("R")
            xdeps = xB.rd()
            p_, pB = next_pg()
            wd = pB.begin()
            last = None
            for c in range(KC):
                s2 = c % 2
                tB = bt(f"ta{s2}")
                tsq = S_.op(ACT, lambda e, c=c, s2=s2: e.activation(out=ta[s2][:].bitcast(BF16)[:, 0:T], in_=xT[:, c, :], func=AF.Square), deps=xdeps + tB.begin())
                tB.wrote(tsq)
                last = S_.op(PE, lambda e, c=c, s2=s2: e.matmul(p_[:], lhsT=onesb[:], rhs=ta[s2][:].bitcast(BF16)[:, 0:T], start=(c == 0), stop=(c == KC - 1)), deps=[tsq] + wd)
                tB.read(last)
            pB.wrote(last)
            rB = bt("rstd")
            tr = S_.op(ACT, lambda e: e.activation(out=rstd[:], in_=p_[:], func=AF.Sqrt, bias=epsc[:, 1:2], scale=1.0 / D), deps=pB.rd() + rB.begin() + CD)
            tr2 = S_.op(DVE, lambda e: e.reciprocal(out=rstd[:], in_=rstd[:]), deps=[tr])
            pB.read(tr)
            rB.wrote(tr2)
            toks = []
            for c in range(KC):
                tk = S_.op(DVE, lambda e, c=c: e.scalar_tensor_tensor(out=xT[:, c, :], in0=xT[:, c, :], scalar=gfin[:, c:c + 1], in1=rstd[:], op0=ALU.mult, op1=ALU.mult),
                           deps=xdeps + [tr2] + CD)
                xB.wrote(tk)
                toks.append(tk)
            rB.read(toks[-1])
            osb = hT[:, :, :].rearrange("p c t -> p (c t)").bitcast(F32)
            hB = bt("hT")
            n_half = (KC * T // 2) // D
            for sub in range(4):
                so = (sub % n_half) * D
                oB = bt(f"osb{sub % n_half}")
                owd = oB.begin() + hB.rd() + list(hB.r.items())
                for c4 in range(0, KC, 4):
                    nn = min(4, KC - c4)
                    p2, p2B = next_pg()
                    wd2 = p2B.begin()
                    last = None
                    for c in range(nn):
                        last = S_.op(PE, lambda e, c=c, c4=c4, p2=p2, sub=sub: e.transpose(out=p2[:, c * 128:(c + 1) * 128], in_=xT[:, c4 + c, sub * 128:(sub + 1) * 128], identity=identf[:]),
                                     deps=xB.rd() + wd2 + CD)
                    p2B.wrote(last)
                    xB.read(last)
                    eng = ACT if (c4 // 4) % 2 else DVE
                    dst_ = osb[:, so + c4 * 128: so + (c4 + nn) * 128]
                    tk = S_.op(eng, (lambda e, p2=p2, dst_=dst_, nn=nn: e.copy(out=dst_, in_=p2[:, 0:nn * 128])) if eng == ACT else
                               (lambda e, p2=p2, dst_=dst_, nn=nn: e.tensor_copy(out=dst_, in_=p2[:, 0:nn * 128])), deps=p2B.rd() + owd)
                    p2B.read(tk)
                    oB.wrote(tk)
                td = S_.dma(SP, f"os{sub % n_half}", out[t0 + sub * 128:t0 + (sub + 1) * 128, :], osb[:, so:so + D], deps=oB.rd())
                oB.read(td)
                hB.read(td)

        def attention(l, i, h, qs_, f_store, v_store, awd, CD):
            t0 = i * T
            qB, kB = bt(f"QT{qs_}"), bt(f"KT{qs_}")
            fs = h % 2
            fqB = bt(f"Fq{fs}")
            tfq = S_.dma(SP, f"fq{fs}", Fq[fs][:], F_scr[l, h, t0:t0 + T].partition_broadcast(128), deps=fqB.begin() + f_store)
            fqB.wrote(tfq)
            items = []
            for j in range(i + 1):
                for kb in range(4):
                    items.append((j, kb))
            nI = len(items)
            loaded = {}

            def load_k(j):
                s = j % 2
                kb_ = bt(f"kbuf{s}")
                tk = S_.dma(SP, f"kl{s}", kbuf[s][:], Kt_scr[l, h][:, j * T:(j + 1) * T], deps=kb_.begin() + ([S_.all_done("kst")] if not S_.dry else []))
                kb_.wrote(tk)

            def load_v(j):
                s = j % 2
                vb_ = bt(f"vbuf{s}")
                tv = S_.dma(SP, f"vl{s}", vbuf[s][:], V_scr[l, h][:, j * 4:(j + 1) * 4, :], deps=vb_.begin() + v_store)
                vb_.wrote(tv)
            if i > 0:
                load_k(0)
                load_v(0)
            pend = []
            oB = pOB
            owd = oB.begin()
            first_pv = [True]
            vB = bt("Vt")
            nB = bt("negF")
            for n in range(nI + 2):
                if n < nI:
                    j, kb = items[n]
                    diag = (j == i)
                    if kb == 0 and j + 1 < i:
                        load_k(j + 1)
                    q0 = kb * 128 if diag else 0
                    s = j % 2
                    if diag:
                        ksrc, ksB = KT[qs_], kB
                    else:
                        ksrc, ksB = kbuf[s], bt(f"kbuf{s}")
                    b2 = n % 2
                    p_, pB = pst[b2], pstB[b2]
                    wd = pB.begin()
                    mm = S_.op(PE, lambda e, p_=p_, ksrc=ksrc, kb=kb, q0=q0: e.matmul(p_[:, q0:T], lhsT=ksrc[:, kb * 128:(kb + 1) * 128], rhs=QT[qs_][:, q0:T], start=True, stop=True),
                               deps=qB.rd() + ksB.rd() + wd)
                    pB.wrote(mm)
                    qB.read(mm); ksB.read(mm)
                    tB = bt(f"tmpS{b2}")
                    t1 = S_.op(DVE, lambda e, p_=p_, b2=b2, q0=q0: e.scalar_tensor_tensor(out=tmpS[b2][:, q0:T], in0=p_[:, q0:T], scalar=SCALE, in1=Fq[fs][:, q0:T],
                                                                                       op0=ALU.mult, op1=ALU.add), deps=[mm, tfq] + tB.begin())
                    pB.read(t1)
                    fqB.read(t1)
                    if diag:
                        t1 = S_.op(DVE, lambda e, b2=b2, q0=q0: e.tensor_tensor(out=tmpS[b2][:, q0:q0 + 128], in0=tmpS[b2][:, q0:q0 + 128], in1=triadd[:], op=ALU.add),
                                   deps=[t1] + CD)
                    tB.wrote(t1)
                    b3 = n % 3
                    ptB = bt(f"pTb{b3}")
                    blk = j * 4 + kb
                    t2 = S_.op(ACT, lambda e, b2=b2, b3=b3, q0=q0, blk=blk: e.activation(out=pTb[b3][:, q0:T], in_=tmpS[b2][:, q0:T], func=AF.Exp,
                                                                                       bias=negF[:, blk, h:h + 1], scale=1.0), deps=[t1] + ptB.begin() + nB.rd())
                    tB.read(t2)
                    nB.read(t2)
                    ptB.wrote(t2)
                    pend.append((j, kb, diag, b3, t2, n))
                if n >= 2 and pend:
                    j, kb, diag, b3, t2, n0 = pend.pop(0)
                    s = j % 2
                    if diag:
                        vsrc = Vt[:, kb, h, 0:128]
                        vsB = vB
                    else:
                        vsrc = vbuf[s][:, kb, 0:128]
                        vsB = bt(f"vbuf{s}")
                    ptB = bt(f"pTb{b3}")
                    q0 = kb * 128 if diag else 0
                    first = (n0 == 0)
                    last_ = (n0 == nI - 1)
                    mm1 = S_.op(PE, lambda e, b3=b3, vsrc=vsrc, q0=q0, first=first, last_=last_: e.matmul(
                        pO[0][:, q0:T], lhsT=vsrc, rhs=pTb[b3][:, q0:T], start=first, stop=last_, skip_group_check=True),
                        deps=[t2] + vsB.rd() + owd)
                    mm2 = S_.op(PE, lambda e, b3=b3, q0=q0, first=first, last_=last_: e.matmul(
                        pO[1][:, q0:T], lhsT=onesb[:], rhs=pTb[b3][:, q0:T], start=first, stop=last_, skip_group_check=True),
                        deps=[t2] + owd + CD)
                    oB.wrote(mm2)
                    ptB.read(mm2)
                    vsB.read(mm1)
                if n < nI and items[n][1] == 1 and items[n][0] + 1 < i:
                    load_v(items[n][0] + 1)

            def finalize():
                aB = bt("aT")
                rcB = bt("tmpS0")
                trec = S_.op(DVE, lambda e: e.reciprocal(out=tmpS[0][:], in_=pO[1][:]), deps=oB.rd() + rcB.begin())
                rcB.wrote(trec)
                tk = S_.op(DVE, lambda e: e.tensor_tensor(out=aT[:, h, :], in0=pO[0][:], in1=tmpS[0][:], op=ALU.mult), deps=[trec] + oB.rd() + awd)
                oB.read(tk)
                rcB.read(tk)
                aB.wrote(tk)
            return finalize

        S_.dry = True
        emit_all()
        S_.dry = False
        B.clear()
        for b_ in pgB + pstB + [pOB] + wslB:
            b_.__init__()
        pgi[0] = 0
        emit_all()
        S_.wait_only(SP, [S_.all_done("os0"), S_.all_done("os1")])
        S_.finish()
    return nc


_CACHE = {}


def _host_inputs(cfg, b, x, c, w_mod, b_mod, g_mix, w_in, b_f, g_v, w_s, b_s, w_pa, w_pm, w_o, g_ffn, w_up, w_down, g_final):
    KC, NH, L = cfg.KC, cfg.NH, cfg.L
    f = lambda a: np.ascontiguousarray(a, dtype=np.float32)
    fm = lambda v, n: f(np.asarray(v).reshape(n, 128).T)
    p = np.arange(128)
    return {
        "x": f(x[b]),
        "cT": fm(c[b], KC),
        "w_mod": f(w_mod),
        "b_modT": f(np.stack([fm(b_mod[l], 6 * KC) for l in range(L)])),
        "g_mixT": f(np.stack([fm(g_mix[l], KC) for l in range(L)])),
        "g_ffnT": f(np.stack([fm(g_ffn[l], KC) for l in range(L)])),
        "g_finT": fm(g_final, KC),
        "w_in": f(w_in),
        "bf_b": f(np.broadcast_to(np.asarray(b_f)[:, None, :], (L, 128, NH))),
        "g_vT": f(np.stack([fm(g_v[l], NH) for l in range(L)])),
        "w_sT": f(np.transpose(np.asarray(w_s), (0, 3, 1, 2))),
        "bs_b": f(np.broadcast_to(np.asarray(b_s)[:, None, :, :], (L, 128, NH, 128))),
        "w_pa": f(w_pa), "w_pm": f(w_pm), "w_o": f(w_o), "w_up": f(w_up), "w_down": f(w_down),
        "c_ident": f(np.eye(128)),
        "c_triadd": f(np.where(p[:, None] > p[None, :], NEG, 0.0)),
        "c_trikeep": f((p[:, None] <= p[None, :]).astype(np.float32)),
    }


def kernel(x, c, w_mod, b_mod, g_mix, w_in, b_f, g_v, w_s, b_s, w_pa, w_pm, w_o, g_ffn, w_up, w_down, g_final):
    x = np.asarray(x)
    Bn, S, D = x.shape
    L = np.asarray(w_mod).shape[0]
    cfg = Cfg(D, S, L)
    key = (D, S, L)
    if key not in _CACHE:
        _CACHE[key] = build(cfg)
    nc = _CACHE[key]
    args = [np.asarray(a) for a in (c, w_mod, b_mod, g_mix, w_in, b_f, g_v, w_s, b_s, w_pa, w_pm, w_o, g_ffn, w_up, w_down, g_final)]
    in_maps = [_host_inputs(cfg, b, x, *args) for b in range(Bn)]
    res = run_bass_kernel_spmd(nc, in_maps, core_ids=list(range(Bn)))
    return np.stack([np.asarray(r["out"]) for r in res.results]).astype(np.float32)
```
